# Optimizing a Trainium2 kernel written in Bass

```python
import jax, jax.numpy as jnp
from jax import lax
import numpy as np

D_MODEL = 1024
BATCH = 4
SEQ = 8192
DEPTH = 1

HEAD_DIM = 64
A_Q_HEADS = 8
A_KV_HEADS = 2
A_HALF_WINDOW = 128
B_HEADS = 8
B_PATTERNS = ((128, 1), (512, 4), (2048, 16))
D_FF = 2816
ROPE_THETA = 10000.0
NORM_EPS = 1e-6
FFN_RES_WEIGHT = 0.5

A_Q_W = A_Q_HEADS * HEAD_DIM
A_KV_W = A_KV_HEADS * HEAD_DIM
B_W = B_HEADS * HEAD_DIM
IN_W = A_Q_W + 2 * A_KV_W + 3 * B_W
MIX_W = A_Q_W + B_W

kernel_name = "hybrid_window_gqa_dilated_macaron_encoder"


def rms_norm(x, g):
    xf = x.astype(jnp.float32)
    y = xf * lax.rsqrt(jnp.mean(xf * xf, axis=-1, keepdims=True) + NORM_EPS)
    return (y * g.astype(jnp.float32)).astype(x.dtype)


def swiglu(h, w_gate, w_up, w_down):
    return (jax.nn.silu(h @ w_gate) * (h @ w_up)) @ w_down


def rope_tables(positions):
    inv_freq = 1.0 / (ROPE_THETA ** (jnp.arange(0, HEAD_DIM, 2, dtype=jnp.float32) / HEAD_DIM))
    ang = positions.astype(jnp.float32)[..., None] * inv_freq
    return jnp.cos(ang)[:, :, None, :], jnp.sin(ang)[:, :, None, :]


def apply_rope(t, cos, sin):
    tf = t.astype(jnp.float32)
    t1, t2 = jnp.split(tf, 2, axis=-1)
    return jnp.concatenate([t1 * cos - t2 * sin, t2 * cos + t1 * sin], axis=-1).astype(t.dtype)


def banded_attention(q, k, v, half_window, sink=None):
    blk = half_window
    B, L, Hq, Dh = q.shape
    Hkv = k.shape[2]
    G = Hq // Hkv
    nb = -(-L // blk)
    Lp = nb * blk
    pad = Lp - L
    qb = jnp.pad(q, ((0, 0), (0, pad), (0, 0), (0, 0))).astype(jnp.float32).reshape(B, nb, blk, Hkv, G, Dh)
    kp = jnp.pad(k, ((0, 0), (blk, blk + pad), (0, 0), (0, 0))).astype(jnp.float32)
    vp = jnp.pad(v, ((0, 0), (blk, blk + pad), (0, 0), (0, 0))).astype(jnp.float32)
    kw = jnp.concatenate([kp[:, j * blk:j * blk + Lp].reshape(B, nb, blk, Hkv, Dh) for j in range(3)], axis=2)
    vw = jnp.concatenate([vp[:, j * blk:j * blk + Lp].reshape(B, nb, blk, Hkv, Dh) for j in range(3)], axis=2)
    qpos = jnp.arange(Lp).reshape(nb, blk)
    kpos = jnp.arange(nb)[:, None] * blk + jnp.arange(3 * blk)[None, :] - blk
    valid = (jnp.abs(qpos[:, :, None] - kpos[:, None, :]) <= half_window) & (kpos[:, None, :] >= 0) & (kpos[:, None, :] < L)
    s = jnp.einsum('bnqhgd,bnkhd->bnhgqk', qb, kw) * (Dh ** -0.5)
    s = jnp.where(valid[None, :, None, None], s, -jnp.inf)
    m = jnp.max(s, axis=-1)
    if sink is not None:
        sk = sink.astype(jnp.float32).reshape(Hkv, G)[None, None, :, :, None]
        m = jnp.maximum(m, sk)
    p = jnp.exp(s - m[..., None])
    den = jnp.sum(p, axis=-1)
    if sink is not None:
        den = den + jnp.exp(sk - m)
    o = jnp.einsum('bnhgqk,bnkhd->bnqhgd', p, vw) / jnp.transpose(den, (0, 1, 4, 2, 3))[..., None]
    lse = jnp.transpose(m + jnp.log(den), (0, 1, 4, 2, 3)).reshape(B, Lp, Hq)[:, :L]
    return o.reshape(B, Lp, Hq, Dh)[:, :L], lse


def dilated_window_attention(q, k, v, window, dilation):
    B, S, H, Dh = q.shape
    msub = S // dilation

    def to_sub(t):
        return t.reshape(B, msub, dilation, H, Dh).transpose(0, 2, 1, 3, 4).reshape(B * dilation, msub, H, Dh)

    o, lse = banded_attention(to_sub(q), to_sub(k), to_sub(v), window // (2 * dilation))
    o = o.reshape(B, dilation, msub, H, Dh).transpose(0, 2, 1, 3, 4).reshape(B, S, H, Dh)
    lse = lse.reshape(B, dilation, msub, H).transpose(0, 2, 1, 3).reshape(B, S, H)
    return o, lse


def mixer(h, w_in, a_sink, w_out, cos, sin):
    B, S, _ = h.shape
    proj = h @ w_in
    cuts = np.cumsum([A_Q_W, A_KV_W, A_KV_W, B_W, B_W]).tolist()
    aq, ak, av, bq, bk, bv = jnp.split(proj, cuts, axis=-1)
    aq = apply_rope(aq.reshape(B, S, A_Q_HEADS, HEAD_DIM), cos, sin)
    ak = apply_rope(ak.reshape(B, S, A_KV_HEADS, HEAD_DIM), cos, sin)
    av = av.reshape(B, S, A_KV_HEADS, HEAD_DIM)
    bq = apply_rope(bq.reshape(B, S, B_HEADS, HEAD_DIM), cos, sin)
    bk = apply_rope(bk.reshape(B, S, B_HEADS, HEAD_DIM), cos, sin)
    bv = bv.reshape(B, S, B_HEADS, HEAD_DIM)
    a_out, _ = banded_attention(aq, ak, av, A_HALF_WINDOW, sink=a_sink)
    outs, lses = [], []
    for w, d in B_PATTERNS:
        o, l = dilated_window_attention(bq, bk, bv, w, d)
        outs.append(o)
        lses.append(l)
    wts = jax.nn.softmax(jnp.stack(lses, axis=0), axis=0)
    b_out = jnp.sum(wts[..., None] * jnp.stack(outs, axis=0), axis=0)
    cat = jnp.concatenate([a_out.reshape(B, S, A_Q_W), b_out.reshape(B, S, B_W)], axis=-1).astype(h.dtype)
    return cat @ w_out


def setup_inputs(seed: int = 0) -> dict:
    key = jax.random.key(seed)
    ks = jax.random.split(key, 16)
    f32 = jnp.float32
    nrm = lambda k, shape, scale: jax.random.normal(k, shape, f32) * scale
    gain = lambda k: 1.0 + 0.02 * jax.random.normal(k, (DEPTH, D_MODEL), f32)
    x = jax.random.normal(ks[0], (BATCH, SEQ, D_MODEL), f32)
    offsets = jax.random.randint(ks[1], (BATCH, 1), 0, 4096, dtype=jnp.int32)
    positions = (jnp.arange(SEQ, dtype=jnp.int32)[None, :] + offsets).astype(jnp.int32)
    return {
        "x": x,
        "positions": positions,
        "norm_ffn1": gain(ks[2]),
        "w_gate1": nrm(ks[3], (DEPTH, D_MODEL, D_FF), D_MODEL ** -0.5),
        "w_up1": nrm(ks[4], (DEPTH, D_MODEL, D_FF), D_MODEL ** -0.5),
        "w_down1": nrm(ks[5], (DEPTH, D_FF, D_MODEL), D_FF ** -0.5),
        "norm_mix": gain(ks[6]),
        "w_in": nrm(ks[7], (DEPTH, D_MODEL, IN_W), D_MODEL ** -0.5),
        "a_sink": nrm(ks[8], (DEPTH, A_Q_HEADS), 0.5),
        "w_out": nrm(ks[9], (DEPTH, MIX_W, D_MODEL), MIX_W ** -0.5),
        "norm_ffn2": gain(ks[10]),
        "w_gate2": nrm(ks[11], (DEPTH, D_MODEL, D_FF), D_MODEL ** -0.5),
        "w_up2": nrm(ks[12], (DEPTH, D_MODEL, D_FF), D_MODEL ** -0.5),
        "w_down2": nrm(ks[13], (DEPTH, D_FF, D_MODEL), D_FF ** -0.5),
        "norm_final": 1.0 + 0.02 * jax.random.normal(ks[14], (D_MODEL,), f32),
    }


def reference(x, positions, norm_ffn1, w_gate1, w_up1, w_down1, norm_mix, w_in, a_sink, w_out,
              norm_ffn2, w_gate2, w_up2, w_down2, norm_final):
    cos, sin = rope_tables(positions)
    for l in range(DEPTH):
        x = x + FFN_RES_WEIGHT * swiglu(rms_norm(x, norm_ffn1[l]), w_gate1[l], w_up1[l], w_down1[l])
        x = x + mixer(rms_norm(x, norm_mix[l]), w_in[l], a_sink[l], w_out[l], cos, sin)
        x = x + FFN_RES_WEIGHT * swiglu(rms_norm(x, norm_ffn2[l]), w_gate2[l], w_up2[l], w_down2[l])
    return rms_norm(x, norm_final)
```

```python
import contextlib
import numpy as np
import ml_dtypes
import concourse.bass as bass
import concourse.mybir as mybir
from concourse.bass_utils import run_bass_kernel_spmd

F32 = mybir.dt.float32
BF16 = mybir.dt.bfloat16
I32 = mybir.dt.int32
ALU = mybir.AluOpType
AF = mybir.ActivationFunctionType

P = 128
DM = 1024
DFF = 2816
NFC = 22
NDC = 8
SEQ = 8192
BATCH = 4
OWN = 4096
LOC = 5120
TT = 512
EPS = 1e-6
INW = 2304
C_AQ = 0
C_AK = 512
C_AV = 768
C_BQ = 900
C_BK = 1412
C_BV = 1924
ROWW = 2444

SB_BYTES = 206 * 1024
ROW32 = SB_BYTES // 4
ROW16 = SB_BYTES // 2


_FENCE = {}


class Tile:
    __slots__ = ("w", "r", "name")

    def __init__(self, name=""):
        self.w = {}
        self.r = dict(_FENCE)
        self.name = name


class Op:
    __slots__ = ("eng", "fn", "deps", "inc", "dom", "val", "is_dma", "idx")


COMPUTE = ("pe", "act", "dve", "pool")
ENGS = ("pe", "act", "dve", "pool", "sp")


class Prog:
    def __init__(self):
        self.ops = {e: [] for e in ENGS}
        self.dom_ops = {}
        self.nlanes = 0

    def lane(self):
        self.nlanes += 1
        return "L%d" % self.nlanes

    def op(self, eng, fn, reads=(), writes=(), lane=None, extra_deps=()):
        o = Op()
        o.eng = eng
        o.fn = fn
        o.is_dma = lane is not None
        o.dom = lane if lane is not None else eng
        o.inc = o.is_dma
        o.val = None
        deps = {}

        def add(p, kind):
            if p.dom == o.dom and not o.is_dma:
                if eng == "pe" or kind != "raw":
                    return
            q = deps.get(p.dom)
            if q is None or p.idx > q.idx:
                deps[p.dom] = p

        for t in reads:
            for p in t.w.values():
                add(p, "raw")
        for t in writes:
            for p in t.w.values():
                add(p, "waw")
            for p in t.r.values():
                add(p, "war")
        for p in extra_deps:
            add(p, "raw")
        o.deps = list(deps.values())
        for p in o.deps:
            p.inc = True
        lst = self.dom_ops.setdefault(o.dom, [])
        o.idx = len(lst)
        lst.append(o)
        self.ops[eng].append(o)
        for t in reads:
            t.r[o.dom] = o
        for t in writes:
            t.w[o.dom] = o
        return o

    def set_fence(self):
        _FENCE.clear()
        for dom, lst in self.dom_ops.items():
            _FENCE[dom] = lst[-1]

    def finalize(self):
        for dom, lst in self.dom_ops.items():
            c = 0
            for o in lst:
                if o.inc:
                    c += 16 if o.is_dma else 1
                    o.val = c

    def emit(self, nc, stack):
        self.finalize()
        sems = {}
        for dom in self.dom_ops:
            sems[dom] = stack.enter_context(nc.semaphore("s_" + dom))
        block = stack.enter_context(nc.Block())
        prog = self

        def run(eng_name):
            def body(e):
                seen = {}
                for o in prog.ops[eng_name]:
                    for d in o.deps:
                        if seen.get(d.dom, 0) >= d.val:
                            continue
                        e.wait_ge(sems[d.dom], d.val)
                        seen[d.dom] = d.val
                    ins = o.fn(e)
                    if o.inc:
                        ins.then_inc(sems[o.dom], 16 if o.is_dma else 1)
            return body

        block.tensor(run("pe"))
        block.scalar(run("act"))
        block.vector(run("dve"))
        block.gpsimd(run("pool"))
        block.sync(run("sp"))


class SbAlloc:
    def __init__(self, limit):
        self.off = 0
        self.limit = limit
        self.marks = []

    def alloc(self, nbytes):
        o = (self.off + 63) // 64 * 64
        self.off = o + nbytes
        assert self.off <= self.limit, "SBUF overflow %d" % self.off
        return o

    def mark(self):
        self.marks.append(self.off)

    def release(self):
        self.off = self.marks.pop()


def build_program(cfg):
    n_tiles1 = cfg.get("n_tiles1", LOC // TT)
    stop_after = cfg.get("stop_after", "all")
    nc = bass.Bass("TRN2", target_bir_lowering=False)
    dr = {}

    def din(name, shape, dt=F32):
        dr[name] = nc.dram_tensor(name, shape, dt, kind="ExternalInput")
        return dr[name]

    x_d = din("x", [LOC, DM])
    pos_d = din("pos", [P, LOC // P], I32)
    invf_d = din("invf", [P, 32])
    ident_d = din("ident", [P, P])
    mask_d = din("masks", [P, 4 * P])
    gains_d = din("gains", [4, DM])
    sink_d = din("a_sink", [1, 8])
    wg1_d = din("w_gate1", [DM, DFF]); wu1_d = din("w_up1", [DM, DFF]); wd1_d = din("w_down1", [DFF, DM])
    wg2_d = din("w_gate2", [DM, DFF]); wu2_d = din("w_up2", [DM, DFF]); wd2_d = din("w_down2", [DFF, DM])
    win_d = din("w_in", [DM, INW])
    wout_d = din("w_out", [DM, DM])
    out_d = nc.dram_tensor("out", [OWN, DM], F32, kind="ExternalOutput")
    dbg = cfg.get("debug", False)
    kind_s = "ExternalOutput" if dbg else "Internal"
    x1_d = nc.dram_tensor("x1s", [LOC, DM], F32, kind=kind_s)
    qkv_d = nc.dram_tensor("qkvs", [LOC, ROWW], BF16, kind=kind_s)
    oa_d = nc.dram_tensor("oas", [OWN, 512], BF16, kind=kind_s)
    ob_d = nc.dram_tensor("obs", [3, OWN, 520], F32, kind=kind_s)

    pg = Prog()
    _FENCE.clear()
    stack = contextlib.ExitStack()
    with stack:
        big32 = stack.enter_context(nc.sbuf_tensor("big", [P, ROW32], F32))
        big16 = big32.bitcast(BF16)
        bigi = big32.bitcast(I32)
        ps32 = stack.enter_context(nc.psum_tensor("ps", [P, 4096], F32))
        ps16 = ps32.bitcast(BF16)

        def A32(off, dims, p0=0, n=P):
            assert off % 4 == 0
            return bass.AP(big32, p0 * ROW32 + off // 4, [[ROW32, n]] + [list(d) for d in dims])

        def A16(off, dims, p0=0, n=P):
            assert off % 2 == 0
            return bass.AP(big16, p0 * ROW16 + off // 2, [[ROW16, n]] + [list(d) for d in dims])

        def AI(off, dims, p0=0, n=P):
            return bass.AP(bigi, p0 * ROW32 + off // 4, [[ROW32, n]] + [list(d) for d in dims])

        def PS32(bank, dims, off=0, p0=0, n=P):
            return bass.AP(ps32, p0 * 4096 + bank * 512 + off, [[4096, n]] + [list(d) for d in dims])

        def PS16(bank, dims, off=0, p0=0, n=P):
            return bass.AP(ps16, p0 * 8192 + bank * 1024 + off, [[8192, n]] + [list(d) for d in dims])

        def DR(t, off, dims):
            return bass.AP(t, off, [list(d) for d in dims])

        sb = SbAlloc(SB_BYTES)
        bankT = [Tile("bank%d" % i) for i in range(8)]

        o_ident = sb.alloc(P * 2)
        o_mask = sb.alloc(4 * P * 2)
        o_small = sb.alloc(1024)
        t_const = Tile("const")
        ident = A16(o_ident, [[1, P]])

        lc = pg.lane()
        pg.op("pool", lambda e: e.dma_start(out=A16(o_ident, [[1, P]]), in_=ident_d.ap()), writes=[t_const], lane=lc)
        lc2 = pg.lane()
        pg.op("pool", lambda e: e.dma_start(out=A16(o_mask, [[1, 4 * P]]), in_=mask_d.ap()), writes=[t_const], lane=lc2)
        def load_gain(k):
            o = sb.alloc(DM * 4)
            t = Tile("gain%d" % k)
            pg.op("sp", lambda e: e.dma_start(out=A32(o, [[1, DM]]), in_=DR(gains_d, k * DM, [[0, P], [1, DM]])),
                  writes=[t], lane=pg.lane())
            return A32(o, [[1, DM]]), t

        small_ctr = [0]

        def small_col():
            c = small_ctr[0] % 256
            small_ctr[0] += 1
            return o_small + 4 * c

        def load_weights_ffn(wg_d, wu_d, wd_d, o_wg, o_wu, o_wd, tiles, parts=("gu", "d"), d_deps=()):
            for g in (range(4) if "gu" in parts else ()):
                for (w_d, o_w, key) in ((wg_d, o_wg, "g"), (wu_d, o_wu, "u")):
                    src = DR(w_d, g * 704, [[DFF, P], [P * DFF, NDC], [1, 704]])
                    dst = A16(o_w + g * 704 * 2, [[DFF, NDC], [1, 704]])
                    pg.op("pool", (lambda e, s=src, d=dst: e.dma_start(out=d, in_=s)),
                          writes=[tiles[key][g]], lane=pg.lane())
            for hf in (range(2) if "d" in parts else ()):
                src = DR(wd_d, hf * 11 * P * DM, [[DM, P], [P * DM, 11], [1, DM]])
                dst = A16(o_wd + hf * 11 * DM * 2, [[DM, 11], [1, DM]])
                pg.op("pool", (lambda e, s=src, d=dst: e.dma_start(out=d, in_=s)),
                      writes=[tiles["d"][hf]], lane=pg.lane(), extra_deps=d_deps)

        def rms_stats(x_ap, t_x, t_stat, col, junk_ap, t_junk):
            ss = A32(col, [[1, 1]])
            pg.op("act", lambda e: e.activation(out=junk_ap, in_=x_ap, func=AF.Square,
                                                scale=1.0 / 32.0, accum_out=ss),
                  reads=[t_x], writes=[t_stat, t_junk])
            pg.op("act", lambda e: e.activation(out=ss, in_=ss, func=AF.Sqrt, bias=EPS, scale=1.0),
                  reads=[t_stat], writes=[t_stat])
            pg.op("dve", lambda e: e.reciprocal(out=ss, in_=ss), reads=[t_stat], writes=[t_stat])
            return ss

        def alloc_load_ffn_weights(wdr, defer=False):
            o_wg = sb.alloc(NDC * DFF * 2)
            o_wu = sb.alloc(NDC * DFF * 2)
            o_wd = sb.alloc(NFC * DM * 2)
            wt = {"g": [Tile() for _ in range(4)], "u": [Tile() for _ in range(4)], "d": [Tile() for _ in range(2)]}
            if defer:
                return (o_wg, o_wu, o_wd, wt), (lambda parts=("gu", "d"), d_deps=(): load_weights_ffn(
                    wdr[0], wdr[1], wdr[2], o_wg, o_wu, o_wd, wt, parts=parts, d_deps=d_deps))
            load_weights_ffn(wdr[0], wdr[1], wdr[2], o_wg, o_wu, o_wd, wt)
            return o_wg, o_wu, o_wd, wt

        def ffn_pass(name, n_tiles, wdr, gain_k, src_d, dst_d, final_gain_k=None, src_dep=None, pre=None, tail_hook=None):
            sb.mark()
            if pre is None:
                o_wg, o_wu, o_wd, wt = alloc_load_ffn_weights(wdr)
            else:
                o_wg, o_wu, o_wd, wt = pre
            o_xn = [sb.alloc(DM * 4) for _ in range(2)]
            o_xr = [sb.alloc(DM * 4) for _ in range(2)]
            o_h = [sb.alloc(DM * 2) for _ in range(2)]
            o_hT = [sb.alloc(NDC * TT * 2) for _ in range(2)]
            o_act = sb.alloc(NFC * TT * 2)
            o_sg = [sb.alloc(TT * 4)] * 2
            o_junk2 = sb.alloc(DM * 2)
            t_junk2 = Tile()
            gain_a, t_gain = load_gain(gain_k)
            if final_gain_k is not None:
                fgain_a, t_fgain = load_gain(final_gain_k)
            t_xn = [Tile() for _ in range(2)]; l_xn = [pg.lane() for _ in range(2)]
            t_xr = [Tile() for _ in range(2)]; l_xr = [pg.lane() for _ in range(2)]
            l_st = [pg.lane() for _ in range(2)]
            t_h = [Tile() for _ in range(2)]
            t_hT = [[Tile() for _ in range(4)] for _ in range(2)]
            t_act = [Tile() for _ in range(NFC)]
            t_sg = [Tile()] * 2
            t_stat = Tile()
            bG = [0, 1]; bU = [2, 3]; bD = [4, 5, 6]; bT = 7
            cnt = {"xn": 0, "xr": 0, "d": 0}

            def norm_sub(i, s, part):
                k = i * 4 + s
                sl = k % 2
                hb = i % 2
                if part == 0:
                    src = DR(src_d, (i * TT + s * P) * DM, [[DM, P], [1, DM]])
                    xa = A32(o_xn[sl], [[1, DM]])
                    pg.op("sp", lambda e: e.dma_start(out=xa, in_=src), writes=[t_xn[sl]], lane=l_xn[sl],
                          extra_deps=([src_dep[k]] if src_dep else ()))
                    col = small_col()
                    ha = A16(o_h[sl], [[1, DM]])
                    ss = rms_stats(xa, t_xn[sl], t_stat, col, ha, t_h[sl])
                    pg.op("dve", lambda e: e.scalar_tensor_tensor(out=ha, in0=xa, scalar=ss, in1=gain_a,
                                                                  op0=ALU.mult, op1=ALU.mult),
                          reads=[t_xn[sl], t_stat, t_gain], writes=[t_h[sl]])
                else:
                    def tr(e):
                        ins = None
                        for dc in range(NDC):
                            ins = e.transpose(out=PS16(bT, [[1, P]], off=dc * P),
                                              in_=A16(o_h[sl] + dc * P * 2, [[1, P]]), identity=ident)
                        return ins
                    pg.op("pe", tr, reads=[t_h[sl], t_const], writes=[bankT[bT]])
                    dst = A16(o_hT[hb] + s * P * 2, [[TT, NDC], [1, P]])
                    pg.op("act", lambda e: e.copy(out=dst, in_=PS16(bT, [[P, NDC], [1, P]])),
                          reads=[bankT[bT]], writes=[t_hT[hb][s]])

            def gate_up(i, f):
                hb = i % 2
                g = f * P // 704
                g2 = (f * P + P - 1) // 704
                wtiles = list({wt["g"][g], wt["g"][g2], wt["u"][g], wt["u"][g2]})

                def mm(e, o_w, bank):
                    ins = None
                    for dc in range(NDC):
                        ins = e.matmul(out=PS32(bank, [[1, TT]]),
                                       lhsT=A16(o_w + (dc * DFF + f * P) * 2, [[1, P]]),
                                       rhs=A16(o_hT[hb] + dc * TT * 2, [[1, TT]]),
                                       start=(dc == 0), stop=(dc == NDC - 1))
                    return ins
                bg = bG[f % 2]; bu = bU[f % 2]
                pg.op("pe", lambda e: mm(e, o_wg, bg), reads=t_hT[hb] + wtiles, writes=[bankT[bg]])
                pg.op("pe", lambda e: mm(e, o_wu, bu), reads=t_hT[hb] + wtiles, writes=[bankT[bu]])
                sg = A32(o_sg[f % 2], [[1, TT]])
                pg.op("act", lambda e: e.activation(out=sg, in_=PS32(bg, [[1, TT]]), func=AF.Silu),
                      reads=[bankT[bg]], writes=[t_sg[f % 2]])
                pg.op("dve", lambda e: e.tensor_tensor(out=A16(o_act + f * TT * 2, [[1, TT]]), in0=sg,
                                                       in1=PS32(bu, [[1, TT]]), op=ALU.mult),
                      reads=[t_sg[f % 2], bankT[bu]], writes=[t_act[f]])

            def down_sub(i, s):
                k = i * 4 + s
                sl = k % 2
                row0 = i * TT + s * P
                src = DR(src_d, row0 * DM, [[DM, P], [1, DM]])
                xr = A32(o_xr[sl], [[1, DM]])
                pg.op("sp", lambda e: e.dma_start(out=xr, in_=src), writes=[t_xr[sl]], lane=l_xr[sl],
                      extra_deps=([src_dep[k]] if src_dep else ()))
                for hf in range(2):
                    bank = bD[cnt["d"] % 3]
                    cnt["d"] += 1

                    def mm(e, bank=bank, hf=hf):
                        ins = None
                        for f in range(NFC):
                            ins = e.matmul(out=PS32(bank, [[1, 512]]),
                                           lhsT=A16(o_act + (f * TT + s * P) * 2, [[1, P]]),
                                           rhs=A16(o_wd + (f * DM + hf * 512) * 2, [[1, 512]]),
                                           start=(f == 0), stop=(f == NFC - 1))
                        return ins
                    pg.op("pe", mm, reads=t_act + wt["d"], writes=[bankT[bank]])
                    xh = A32(o_xr[sl] + hf * 2048, [[1, 512]])
                    pg.op("dve", lambda e, bank=bank, xh=xh: e.scalar_tensor_tensor(
                        out=xh, in0=PS32(bank, [[1, 512]]), scalar=0.5, in1=xh, op0=ALU.mult, op1=ALU.add),
                        reads=[bankT[bank], t_xr[sl]], writes=[t_xr[sl]])
                if final_gain_k is not None:
                    col = small_col()
                    ss = rms_stats(xr, t_xr[sl], t_stat, col, A16(o_junk2, [[1, DM]]), t_junk2)
                    pg.op("dve", lambda e: e.scalar_tensor_tensor(out=xr, in0=xr, scalar=ss, in1=fgain_a,
                                                                  op0=ALU.mult, op1=ALU.mult),
                          reads=[t_xr[sl], t_stat, t_fgain], writes=[t_xr[sl]])
                dst = DR(dst_d, row0 * DM, [[DM, P], [1, DM]])
                return pg.op("pool", lambda e: e.dma_start(out=dst, in_=xr), reads=[t_xr[sl]], lane=l_st[sl])

            stores = []
            for s in range(4):
                norm_sub(0, s, 0)
                norm_sub(0, s, 1)
            for i in range(n_tiles):
                for f in range(NFC):
                    gate_up(i, f)
                    if i + 1 < n_tiles:
                        if f in (1, 6, 11, 16):
                            norm_sub(i + 1, (f - 1) // 5, 0)
                        if f in (5, 10, 15, 20):
                            norm_sub(i + 1, (f - 5) // 5, 1)
                if tail_hook is not None and i == n_tiles - 1:
                    tail_hook(o_wg, pg.ops["pe"][-1])
                for s in range(4):
                    stores.append(down_sub(i, s))
            sb.release()
            pg.set_fence()
            return stores

        win_pre = {}

        def load_win(o_win, extra):
            t_win = [Tile(), Tile()]
            for hf in range(2):
                src = DR(win_d, hf * 1152, [[INW, P], [P * INW, NDC], [1, 1152]])
                dst = A16(o_win + hf * 1152 * 2, [[INW, NDC], [1, 1152]])
                pg.op("pool", (lambda e, s=src, d=dst: e.dma_start(out=d, in_=s)), writes=[t_win[hf]], lane=pg.lane(),
                      extra_deps=extra)
            return t_win

        def win_hook(o_wg, last_pe_op):
            win_pre["o"] = o_wg
            win_pre["t"] = load_win(o_wg, [last_pe_op])

        st1 = ffn_pass("ffn1", n_tiles1, (wg1_d, wu1_d, wd1_d), 0, x_d, x1_d,
                       tail_hook=(win_hook if cfg.get("win_prefetch", True) and stop_after != "ffn1" else None))
        n_sub1 = n_tiles1 * 4

        def phase1b(n_sub, x1_stores):
            import math
            sb.mark()
            o_win = sb.alloc(NDC * INW * 2)
            if "o" in win_pre:
                assert win_pre["o"] == o_win, (win_pre["o"], o_win)
                t_win = win_pre["t"]
            else:
                t_win = load_win(o_win, [])
            gain_a, t_gain = load_gain(1)
            NS = LOC // P
            o_pos = sb.alloc(NS * 4); o_posf = sb.alloc(NS * 4); o_invf = sb.alloc(32 * 4)
            o_ang = sb.alloc(NS * 32 * 4); o_u = sb.alloc(NS * 32 * 4)
            o_cos = sb.alloc(NS * 64 * 4); o_sin = sb.alloc(NS * 64 * 4)
            t_rope = Tile()
            pg.op("sp", lambda e: e.dma_start(out=AI(o_pos, [[1, NS]]), in_=pos_d.ap()), writes=[t_rope], lane=pg.lane())
            pg.op("sp", lambda e: e.dma_start(out=A32(o_invf, [[1, 32]]), in_=invf_d.ap()), writes=[t_rope], lane=pg.lane())
            pg.op("dve", lambda e: e.tensor_copy(A32(o_posf, [[1, NS]]), AI(o_pos, [[1, NS]])), reads=[t_rope], writes=[t_rope])
            pg.op("dve", lambda e: e.tensor_tensor(out=A32(o_ang, [[32, NS], [1, 32]]), in0=A32(o_posf, [[1, NS], [0, 32]]),
                                                   in1=A32(o_invf, [[0, NS], [1, 32]]), op=ALU.mult),
                  reads=[t_rope], writes=[t_rope])
            C1 = 6.28125
            C2 = 2 * math.pi - C1
            o_v = sb.alloc(NS * 32 * 4); o_ki = sb.alloc(NS * 32 * 4); o_kf = sb.alloc(NS * 32 * 4); o_m = sb.alloc(NS * 32 * 4)
            NE = NS * 32

            def reduce_angle(shift):
                fl = lambda o: A32(o, [[1, NE]])
                pg.op("dve", lambda e: e.tensor_scalar(out=fl(o_v), in0=fl(o_ang), scalar1=1.0 / (2 * math.pi),
                                                       scalar2=0.5 + shift / (2 * math.pi), op0=ALU.mult, op1=ALU.add),
                      reads=[t_rope], writes=[t_rope])
                pg.op("dve", lambda e: e.tensor_copy(AI(o_ki, [[1, NE]]), fl(o_v)), reads=[t_rope], writes=[t_rope])
                pg.op("dve", lambda e: e.tensor_copy(fl(o_kf), AI(o_ki, [[1, NE]])), reads=[t_rope], writes=[t_rope])
                pg.op("dve", lambda e: e.scalar_tensor_tensor(out=fl(o_u), in0=fl(o_kf), scalar=-C1, in1=fl(o_ang),
                                                              op0=ALU.mult, op1=ALU.add), reads=[t_rope], writes=[t_rope])
                pg.op("dve", lambda e: e.scalar_tensor_tensor(out=fl(o_u), in0=fl(o_kf), scalar=-C2, in1=fl(o_u),
                                                              op0=ALU.mult, op1=ALU.add), reads=[t_rope], writes=[t_rope])
                pg.op("dve", lambda e: e.tensor_scalar(out=fl(o_m), in0=fl(o_u), scalar1=-math.pi - shift, scalar2=None,
                                                       op0=ALU.is_lt), reads=[t_rope], writes=[t_rope])
                pg.op("dve", lambda e: e.scalar_tensor_tensor(out=fl(o_u), in0=fl(o_m), scalar=2 * math.pi, in1=fl(o_u),
                                                              op0=ALU.mult, op1=ALU.add), reads=[t_rope], writes=[t_rope])

            reduce_angle(0.0)
            pg.op("act", lambda e: e.activation(out=A32(o_sin + 32 * 4, [[64, NS], [1, 32]]), in_=A32(o_u, [[32, NS], [1, 32]]),
                                                func=AF.Sin, bias=0.0, scale=1.0), reads=[t_rope], writes=[t_rope])
            pg.op("act", lambda e: e.activation(out=A32(o_sin, [[64, NS], [1, 32]]), in_=A32(o_u, [[32, NS], [1, 32]]),
                                                func=AF.Sin, bias=0.0, scale=-1.0), reads=[t_rope], writes=[t_rope])
            reduce_angle(math.pi / 2)
            pg.op("act", lambda e: e.activation(out=A32(o_cos, [[64, NS], [1, 32]]), in_=A32(o_u, [[32, NS], [1, 32]]),
                                                func=AF.Sin, bias=math.pi / 2, scale=1.0), reads=[t_rope], writes=[t_rope])
            pg.op("act", lambda e: e.activation(out=A32(o_cos + 32 * 4, [[64, NS], [1, 32]]), in_=A32(o_u, [[32, NS], [1, 32]]),
                                                func=AF.Sin, bias=math.pi / 2, scale=1.0), reads=[t_rope], writes=[t_rope])

            o_xa = [sb.alloc(DM * 4) for _ in range(2)]
            o_hb = [sb.alloc(DM * 2) for _ in range(2)]
            o_hT2 = [sb.alloc(NDC * P * 2) for _ in range(2)]
            o_tA = [sb.alloc(512 * 4) for _ in range(2)]
            o_tB = [sb.alloc(512 * 4) for _ in range(2)]
            o_row = [sb.alloc(ROWW * 2) for _ in range(2)]
            t_xa = [Tile(), Tile()]; l_xa = [pg.lane(), pg.lane()]
            t_hb = [Tile(), Tile()]; t_hT2 = [Tile(), Tile()]
            t_tA = [Tile(), Tile()]; t_tB = [Tile(), Tile()]
            t_row = [Tile(), Tile()]; l_row = [pg.lane(), pg.lane()]
            t_stat = Tile()
            for sl in range(2):
                pg.op("pool", lambda e, sl=sl: e.memset(A16(o_row[sl] + (C_AV + 64) * 2, [[65, 2], [1, 1]]), 1.0), writes=[t_row[sl]])
                pg.op("pool", lambda e, sl=sl: e.memset(A16(o_row[sl] + (C_BV + 64) * 2, [[65, 8], [1, 1]]), 1.0), writes=[t_row[sl]])
            bT = 7
            grp = [(0, 512), (512, 512), (1024, 512), (1536, 512), (2048, 256)]
            cntj = [0]
            stores = []
            def stage_a(k):
                sl = k % 2
                src = DR(x1_d, k * P * DM, [[DM, P], [1, DM]])
                xa = A32(o_xa[sl], [[1, DM]])
                pg.op("sp", lambda e: e.dma_start(out=xa, in_=src), writes=[t_xa[sl]], lane=l_xa[sl],
                      extra_deps=[x1_stores[k]])
                ha = A16(o_hb[sl], [[1, DM]])
                ss = rms_stats(xa, t_xa[sl], t_stat, small_col(), ha, t_hb[sl])
                pg.op("dve", lambda e: e.scalar_tensor_tensor(out=ha, in0=xa, scalar=ss, in1=gain_a,
                                                              op0=ALU.mult, op1=ALU.mult),
                      reads=[t_xa[sl], t_stat, t_gain], writes=[t_hb[sl]])

            def stage_t(k):
                sl = k % 2

                def tr(e):
                    ins = None
                    for dc in range(NDC):
                        ins = e.transpose(out=PS16(bT, [[1, P]], off=dc * P),
                                          in_=A16(o_hb[sl] + dc * P * 2, [[1, P]]), identity=ident)
                    return ins
                pg.op("pe", tr, reads=[t_hb[sl], t_const], writes=[bankT[bT]])
                pg.op("act", lambda e: e.copy(out=A16(o_hT2[sl], [[1, NDC * P]]), in_=PS16(bT, [[1, NDC * P]])),
                      reads=[bankT[bT]], writes=[t_hT2[sl]])

            def stage_mm(k, gis):
                sl = k % 2
                for gi in gis:
                    c0, cw = grp[gi]

                    def mm(e, gi=gi, c0=c0, cw=cw):
                        ins = None
                        for dc in range(NDC):
                            ins = e.matmul(out=PS32(gi, [[1, cw]]),
                                           lhsT=A16(o_hT2[sl] + dc * P * 2, [[1, P]]),
                                           rhs=A16(o_win + (dc * INW + c0) * 2, [[1, cw]]),
                                           start=(dc == 0), stop=(dc == NDC - 1))
                        return ins
                    pg.op("pe", mm, reads=[t_hT2[sl]] + t_win, writes=[bankT[gi]])

            def stage_c(k):
                sl = k % 2
                cosk = o_cos + k * 64 * 4
                sink = o_sin + k * 64 * 4

                def rope(bank, off, H, dst_c, dup=False):
                    j = cntj[0] % 2
                    cntj[0] += 1
                    tA = A32(o_tA[j], [[64, H], [1, 64]])
                    pg.op("dve", lambda e: e.tensor_tensor(out=tA, in0=PS32(bank, [[64, H], [1, 64]], off=off),
                                                           in1=A32(cosk, [[0, H], [1, 64]]), op=ALU.mult),
                          reads=[bankT[bank], t_rope], writes=[t_tA[j]])
                    pg.op("dve", lambda e: e.tensor_tensor(out=A32(o_tB[j], [[64, H], [1, 32]]),
                                                           in0=PS32(bank, [[64, H], [1, 32]], off=off + 32),
                                                           in1=A32(sink, [[0, H], [1, 32]]), op=ALU.mult),
                          reads=[bankT[bank], t_rope], writes=[t_tB[j]])
                    pg.op("dve", lambda e: e.tensor_tensor(out=A32(o_tB[j] + 32 * 4, [[64, H], [1, 32]]),
                                                           in0=PS32(bank, [[64, H], [1, 32]], off=off),
                                                           in1=A32(sink + 32 * 4, [[0, H], [1, 32]]), op=ALU.mult),
                          reads=[bankT[bank], t_rope], writes=[t_tB[j]])
                    if not dup:
                        pg.op("pool", lambda e: e.tensor_tensor(out=A16(o_row[sl] + dst_c * 2, [[64, H], [1, 64]]), in0=tA,
                                                                in1=A32(o_tB[j], [[64, H], [1, 64]]), op=ALU.add),
                              reads=[t_tA[j], t_tB[j]], writes=[t_row[sl]])
                    else:
                        for du in range(2):
                            pg.op("pool", lambda e, du=du: e.tensor_tensor(
                                out=A16(o_row[sl] + (C_AK + du * 64) * 2, [[128, H], [1, 64]]), in0=tA,
                                in1=A32(o_tB[j], [[64, H], [1, 64]]), op=ALU.add),
                                reads=[t_tA[j], t_tB[j]], writes=[t_row[sl]])

                rope(0, 0, 8, C_AQ)
                rope(1, 0, 2, None, dup=True)
                rope(1, 256, 4, C_BQ)
                pg.op("act", lambda e: e.copy(out=A16(o_row[sl] + C_AV * 2, [[65, 2], [1, 64]]),
                                              in_=PS32(1, [[64, 2], [1, 64]], off=128)),
                      reads=[bankT[1]], writes=[t_row[sl]])
                rope(2, 0, 8, C_BQ + 256)
                rope(3, 0, 4, C_BK + 256)
                pg.op("act", lambda e: e.copy(out=A16(o_row[sl] + C_BV * 2, [[65, 4], [1, 64]]),
                                              in_=PS32(3, [[64, 4], [1, 64]], off=256)),
                      reads=[bankT[3]], writes=[t_row[sl]])
                pg.op("act", lambda e: e.copy(out=A16(o_row[sl] + (C_BV + 4 * 65) * 2, [[65, 4], [1, 64]]),
                                              in_=PS32(4, [[64, 4], [1, 64]], off=0)),
                      reads=[bankT[4]], writes=[t_row[sl]])
                dst = DR(qkv_d, k * P * ROWW, [[ROWW, P], [1, ROWW]])
                stores.append(pg.op("pool", lambda e: e.dma_start(out=dst, in_=A16(o_row[sl], [[1, ROWW]])),
                                    reads=[t_row[sl]], lane=l_row[sl]))

            stage_a(0)
            stage_t(0)
            for k in range(n_sub):
                if k + 1 < n_sub:
                    stage_a(k + 1)
                stage_mm(k, [0, 1, 2])
                if k + 1 < n_sub:
                    stage_t(k + 1)
                stage_mm(k, [3, 4])
                stage_c(k)
            sb.release()
            pg.set_fence()
            return stores

        st1b = []
        if stop_after != "ffn1":
            st1b = phase1b(n_sub1, st1)

        def phase2(qkv_stores):
            sb.mark()
            NKB = 12
            NQB = 12
            RAWK = 1032
            QTW = 4 * NQB * P
            o_es = sb.alloc(8 * 4)
            t_es = Tile()
            pg.op("sp", lambda e: e.dma_start(out=A32(o_es, [[1, 8]]), in_=DR(sink_d, 0, [[0, P], [1, 8]])),
                  writes=[t_es], lane=pg.lane())
            pg.op("act", lambda e: e.activation(out=A32(o_es, [[1, 8]]), in_=A32(o_es, [[1, 8]]), func=AF.Exp),
                  reads=[t_es], writes=[t_es])
            o_kv = [sb.alloc(NKB * RAWK * 2) for _ in range(2)]
            o_q = [sb.alloc(NQB * 512 * 2) for _ in range(2)]
            o_kT = [sb.alloc(4 * NKB * P * 2) for _ in range(2)]
            o_qT = [sb.alloc(2 * QTW * 2) for _ in range(2)]
            NPT = cfg.get("npt", 6)
            o_pT = [sb.alloc(6 * P * 2) for _ in range(NPT)]
            NOB = 4
            o_ob = [sb.alloc(520 * 4) for _ in range(NOB)]
            o_ca = [sb.alloc(512 * 2) for _ in range(NOB)]
            o_den = sb.alloc(16 * 4)
            t_kv = [Tile(), Tile()]; t_q = [Tile(), Tile()]
            l_kv = [[pg.lane(), pg.lane()] for _ in range(2)]; l_q = [[pg.lane(), pg.lane()] for _ in range(2)]
            t_kT = [Tile(), Tile()]; t_qT = [Tile(), Tile()]
            t_pT = [Tile() for _ in range(NPT)]
            t_ob = [Tile() for _ in range(NOB)]; l_ob = [pg.lane() for _ in range(NOB)]
            t_ca = [Tile() for _ in range(NOB)]; l_ca = [pg.lane() for _ in range(NOB)]
            t_den = Tile()
            t_fill = Tile()
            NFILL = cfg.get("fill", 0)
            bS = [(0, 1), (2, 3)]; bO = [(4, 5), (6, 4), (5, 6)]; bT = 7
            ctr = {"S": 0, "pT": 0, "ob": 0, "ca": 0, "m": 0, "qb": 0, "T": 0}
            bTs = (7, 3)
            out_stores = []
            for b in range(2):
                pg.op("pool", lambda e, b=b: e.memset(A16(o_kv[b], [[1, NKB * RAWK]]), 0.0), writes=[t_kv[b]])
                pg.op("pool", lambda e, b=b: e.memset(A16(o_q[b], [[1, NQB * 512]]), 0.0), writes=[t_q[b]])
                pg.op("pool", lambda e, b=b: e.memset(A16(o_kT[b], [[1, 4 * NKB * P]]), 0.0), writes=[t_kT[b]])
                pg.op("pool", lambda e, b=b: e.memset(A16(o_qT[b], [[1, 2 * QTW]]), 0.0), writes=[t_qT[b]])

            segs = []
            for s_ in range(4):
                segs.append([("A", 1, 0, 8 * s_, 8 * s_ + 8, None)])
            for (n0, n1) in ((0, 9), (9, 18), (18, 27), (27, 33)):
                segs.append([("B", 1, 0, n0, n1, 0)])
            for r in range(4):
                segs.append([("B", 4, r, 0, 9, 1)])
            for r in range(0, 16, 4):
                segs.append([("B", 16, r + i_, 0, 3, 2) for i_ in range(4)])

            def seg_geom(seg):
                kind, D, r, n0, n1, pat = seg
                Lk = LOC // D
                Lq = OWN // D
                if kind == "A":
                    j0 = max(0, n0 - 1); j1 = n1
                    q_lo = P * n0
                else:
                    j0 = max(0, n0 - 1); j1 = n1 - 1
                    q_lo = P * n0 - 64
                j1 = min(j1, (Lk - 1) // P)
                return Lk, Lq, j0, j1, q_lo

            def part_prepare(seg, buf, kb_base, qb_base):
                kind, D, r, n0, n1, pat = seg
                Lk, Lq, j0, j1, q_lo = seg_geom(seg)
                if kind == "A":
                    kc0, kcols, nkc, qc0 = C_AK, 256 + 130, 2, C_AQ
                else:
                    kc0, kcols, nkc, qc0 = C_BK, 512 + 520, 4, C_BQ
                k_lo = P * j0
                k_hi = min(Lk, P * (j1 + 1))
                nfull = (k_hi - k_lo) // P
                krem = (k_hi - k_lo) % P
                tok0 = r + D * k_lo
                if nfull:
                    src = DR(qkv_d, tok0 * ROWW + kc0, [[D * ROWW, P], [P * D * ROWW, nfull], [1, kcols]])
                    dst = A16(o_kv[buf] + kb_base * RAWK * 2, [[RAWK, nfull], [1, kcols]])
                    pg.op("sp", lambda e, dst=dst, src=src: e.dma_start(out=dst, in_=src), writes=[t_kv[buf]],
                          lane=l_kv[buf][0], extra_deps=qkv_stores)
                if krem:
                    src2 = DR(qkv_d, (tok0 + D * P * nfull) * ROWW + kc0, [[D * ROWW, krem], [1, kcols]])
                    dst2 = A16(o_kv[buf] + (kb_base + nfull) * RAWK * 2, [[1, kcols]], n=krem)
                    pg.op("sp", lambda e, dst2=dst2, src2=src2: e.dma_start(out=dst2, in_=src2), writes=[t_kv[buf]],
                          lane=l_kv[buf][1], extra_deps=qkv_stores)
                nkb = nfull + (1 if krem else 0)
                nqb = n1 - n0
                jq0 = 0
                if q_lo < 0:
                    src2 = DR(qkv_d, r * ROWW + qc0, [[D * ROWW, 64], [1, 512]])
                    dst2 = A16(o_q[buf] + qb_base * 512 * 2, [[1, 512]], p0=64, n=64)
                    pg.op("sp", lambda e, dst2=dst2, src2=src2: e.dma_start(out=dst2, in_=src2), writes=[t_q[buf]],
                          lane=l_q[buf][1], extra_deps=qkv_stores)
                    jq0 = 1
                if nqb - jq0 > 0:
                    qtok0 = r + D * (q_lo + P * jq0)
                    src = DR(qkv_d, qtok0 * ROWW + qc0, [[D * ROWW, P], [P * D * ROWW, nqb - jq0], [1, 512]])
                    dst = A16(o_q[buf] + (qb_base + jq0) * 512 * 2, [[512, nqb - jq0], [1, 512]])
                    pg.op("sp", lambda e, dst=dst, src=src: e.dma_start(out=dst, in_=src), writes=[t_q[buf]],
                          lane=l_q[buf][0], extra_deps=qkv_stores)
                yield
                for jj in range(nkb):
                    bT = bTs[ctr["T"] % 2]; ctr["T"] += 1

                    def trk(e, jj=jj, bT=bT):
                        ins = None
                        for c in range(nkc):
                            ins = e.transpose(out=PS16(bT, [[1, P]], off=c * P),
                                              in_=A16(o_kv[buf] + ((kb_base + jj) * RAWK + c * P) * 2, [[1, P]]), identity=ident)
                        return ins
                    pg.op("pe", trk, reads=[t_kv[buf], t_const], writes=[bankT[bT]])
                    if cfg.get("kevac_act", True):
                        pg.op("act", lambda e, jj=jj, bT=bT: e.copy(
                            out=A16(o_kT[buf] + (kb_base + jj) * P * 2, [[NKB * P, nkc], [1, P]]), in_=PS16(bT, [[P, nkc], [1, P]])),
                            reads=[bankT[bT]], writes=[t_kT[buf]])
                    else:
                        pg.op("dve", lambda e, jj=jj, bT=bT: e.tensor_copy(
                            A16(o_kT[buf] + (kb_base + jj) * P * 2, [[NKB * P, nkc], [1, P]]), PS16(bT, [[P, nkc], [1, P]])),
                            reads=[bankT[bT]], writes=[t_kT[buf]])
                    yield
                for jj in range(nqb):
                    bT = bTs[ctr["T"] % 2]; ctr["T"] += 1

                    def trq(e, jj=jj, bT=bT):
                        ins = None
                        for c in range(4):
                            ins = e.transpose(out=PS16(bT, [[1, P]], off=c * P),
                                              in_=A16(o_q[buf] + ((qb_base + jj) * 512 + c * P) * 2, [[1, P]]), identity=ident)
                        return ins
                    pg.op("pe", trq, reads=[t_q[buf], t_const], writes=[bankT[bT]])
                    for e_ in range(2):
                        pg.op("dve", lambda e, jj=jj, e_=e_, bT=bT: e.tensor_copy(
                            A16(o_qT[buf] + (e_ * QTW + (qb_base + jj) * P) * 2, [[NQB * P, 4], [1, P]], p0=64 * e_, n=64),
                            PS16(bT, [[P, 4], [1, P]], p0=64 * e_, n=64)),
                            reads=[bankT[bT]], writes=[t_qT[buf]])
                    yield

            def part_units(seg, buf, kb_base, qb_base):
                kind, D, r, n0, n1, pat = seg
                Lk, Lq, j0, j1, q_lo = seg_geom(seg)
                for n in range(n0, n1):
                    if kind == "A":
                        qs = P * n
                        blocks = [(jb, mk) for jb, mk in ((n - 1, 0), (n, None), (n + 1, 1)) if jb >= 0]
                        v0, v1 = 0, P
                    else:
                        qs = P * n - 64
                        blocks = [(jb, mk) for jb, mk in ((n - 1, 2), (n, 3)) if jb >= 0 and P * jb < Lk]
                        v0 = 64 if qs < 0 else 0
                        v1 = 64 if qs + P > Lq else P
                    qcol = qs - q_lo + qb_base * P
                    bOs = bO[ctr["qb"] % 3]
                    ctr["qb"] += 1
                    for c in range(4):
                        yield unit(kind, D, r, pat, buf, j0 - kb_base, blocks, qcol, qs, c, bOs, v0, v1)

            def unit(kind, D, r, pat, buf, j0, blocks, qcol, qs, c, bOs, v0, v1):
                nb = len(blocks)
                kc = (c // 2) if kind == "A" else c
                st = {}

                def front():
                    if nb > 2:
                        bss = ((0, 1), (1, 2))[ctr["S"] % 2]
                    else:
                        b1 = ctr["S"] % 3
                        bss = (b1, b1)
                    ctr["S"] += 1
                    pi = ctr["pT"] % NPT; ctr["pT"] += 1
                    st["pi"] = pi
                    sb0 = bss[0]

                    def score(e):
                        ins = None
                        for i, (jb, mk) in enumerate(blocks):
                            ins = e.matmul(out=PS32(sb0, [[1, 2 * P]], off=i * 2 * P),
                                           lhsT=A16(o_kT[buf] + (kc * NKB * P + (jb - j0) * P) * 2, [[1, P]]),
                                           rhs=A16(o_qT[buf] + (c * NQB * P + qcol) * 2, [[QTW, 2], [1, P]]),
                                           start=True, stop=True)
                        return ins
                    sbanks = [bankT[bss[0]]] + ([bankT[bss[1]]] if nb > 2 else [])
                    pg.op("pe", score, reads=[t_kT[buf], t_qT[buf]], writes=sbanks)
                    pg.op("act", lambda e: e.activation(
                        out=A16(o_pT[pi], [[1, nb * 2 * P]]), in_=PS32(sb0, [[1, nb * 2 * P]]), func=AF.Exp, scale=0.125),
                        reads=sbanks, writes=[t_pT[pi]])
                    mlist = [(i, mk) for i, (jb, mk) in enumerate(blocks) if mk is not None]
                    meng = cfg.get("mask_engs", ("dve",))[ctr["m"] % len(cfg.get("mask_engs", ("dve",)))]
                    ctr["m"] += 1
                    if len(mlist) == 2:
                        i0, mk0 = mlist[0]
                        i1, mk1 = mlist[1]
                        assert mk1 == mk0 + 1
                        pa = A16(o_pT[pi] + i0 * 2 * P * 2, [[(i1 - i0) * 2 * P, 2], [P, 2], [1, P]])
                        ma = A16(o_mask + mk0 * P * 2, [[P, 2], [0, 2], [1, P]])
                    else:
                        i0, mk0 = mlist[0]
                        pa = A16(o_pT[pi] + i0 * 2 * P * 2, [[P, 2], [1, P]])
                        ma = A16(o_mask + mk0 * P * 2, [[0, 2], [1, P]])
                    pg.op(meng, lambda e: e.tensor_tensor(out=pa, in0=pa, in1=ma, op=ALU.mult),
                          reads=[t_pT[pi], t_const], writes=[t_pT[pi]])

                def back():
                    pi = st["pi"]

                    def pv(e):
                        ins = None
                        for hi in range(2):
                            h = 2 * c + hi
                            vh = (h // 4) if kind == "A" else h
                            voff = (256 if kind == "A" else 512) + vh * 65
                            bo = bOs[h // 4]
                            for i, (jb, mk) in enumerate(blocks):
                                ins = e.matmul(out=PS32(bo, [[1, 65]], off=(h % 4) * 65),
                                               lhsT=A16(o_pT[pi] + (i * 2 + hi) * P * 2, [[1, P]]),
                                               rhs=A16(o_kv[buf] + ((jb - j0) * RAWK + voff) * 2, [[1, 65]]),
                                               start=(i == 0), stop=(i == nb - 1))
                        return ins
                    pg.op("pe", pv, reads=[t_pT[pi], t_kv[buf]], writes=[bankT[bOs[c // 2]]])
                    if False:
                        def fill(e):
                            ins = None
                            for _ in range(NFILL):
                                ins = e.matmul(out=PS32(3, [[1, 4 * P]]), lhsT=ident, rhs=A16(o_mask, [[1, 4 * P]]),
                                               start=True, stop=True)
                            return ins
                        pg.op("pe", fill, reads=[t_const], writes=[t_fill])
                    if c != 3:
                        return
                    nv = v1 - v0
                    if kind == "A":
                        ci = ctr["ca"] % NOB; ctr["ca"] += 1
                        for b2 in range(2):
                            pg.op("dve", lambda e, b2=b2: e.tensor_tensor(
                                out=A32(o_den + b2 * 16, [[1, 4]]), in0=PS32(bOs[b2], [[65, 4]], off=64),
                                in1=A32(o_es + b2 * 16, [[1, 4]]), op=ALU.add),
                                reads=[bankT[bOs[b2]], t_es], writes=[t_den])
                        pg.op("dve", lambda e: e.reciprocal(out=A32(o_den + 32, [[1, 8]]), in_=A32(o_den, [[1, 8]])),
                              reads=[t_den], writes=[t_den])
                        for b2 in range(2):
                            pg.op("dve", lambda e, b2=b2: e.tensor_tensor(
                                out=A16(o_ca[ci] + b2 * 256 * 2, [[64, 4], [1, 64]]),
                                in0=PS32(bOs[b2], [[65, 4], [1, 64]]),
                                in1=A32(o_den + 32 + b2 * 16, [[1, 4], [0, 64]]), op=ALU.mult),
                                reads=[bankT[bOs[b2]], t_den], writes=[t_ca[ci]])
                        dst = DR(oa_d, qs * 512, [[512, P], [1, 512]])
                        out_stores.append(pg.op(cfg.get("st_eng", "pool"), lambda e: e.dma_start(
                            out=dst, in_=A16(o_ca[ci], [[1, 512]])), reads=[t_ca[ci]], lane=l_ca[ci]))
                    else:
                        oi = ctr["ob"] % NOB; ctr["ob"] += 1
                        for b2 in range(2):
                            pg.op("dve", lambda e, b2=b2: e.tensor_copy(
                                A32(o_ob[oi] + b2 * 260 * 4, [[1, 260]]), PS32(bOs[b2], [[1, 260]])),
                                reads=[bankT[bOs[b2]]], writes=[t_ob[oi]])
                        dst = DR(ob_d, (pat * OWN + r + D * (qs + v0)) * 520, [[D * 520, nv], [1, 520]])
                        out_stores.append(pg.op(cfg.get("st_eng", "pool"), lambda e: e.dma_start(
                            out=dst, in_=A32(o_ob[oi], [[1, 520]], p0=v0, n=nv)), reads=[t_ob[oi]], lane=l_ob[oi]))
                return front, back

            def part_sizes(seg):
                kind, D, r, n0, n1, pat = seg
                Lk, Lq, j0, j1, q_lo = seg_geom(seg)
                return (j1 - j0 + 1), (n1 - n0)

            def seg_bases(parts):
                kb = qb = 0
                out = []
                for p_ in parts:
                    out.append((kb, qb))
                    nk, nq_ = part_sizes(p_)
                    kb += nk; qb += nq_
                assert kb <= NKB and qb <= NQB, (kb, qb)
                return out

            def seg_prepare(parts, buf):
                for p_, (kb, qb) in zip(parts, seg_bases(parts)):
                    yield from part_prepare(p_, buf, kb, qb)

            def seg_units(parts, buf):
                for p_, (kb, qb) in zip(parts, seg_bases(parts)):
                    yield from part_units(p_, buf, kb, qb)

            n_seg = cfg.get("n_seg", len(segs))
            segs = segs[:n_seg] if isinstance(n_seg, int) else [segs[i] for i in n_seg]
            LAG = cfg.get("lag", 4)
            prep = seg_prepare(segs[0], 0)
            for _ in prep:
                pass
            pend = []
            for si, seg in enumerate(segs):
                while pend:
                    pend.pop(0)()
                nxt = seg_prepare(segs[si + 1], (si + 1) % 2) if si + 1 < len(segs) else None
                for (front, back) in seg_units(seg, si % 2):
                    if nxt is not None:
                        next(nxt, None)
                    front()
                    pend.append(back)
                    if len(pend) > LAG:
                        pend.pop(0)()
                if nxt is not None:
                    for _ in nxt:
                        pass
            while pend:
                pend.pop(0)()
            sb.release()
            pg.set_fence()
            return out_stores

        st2 = []
        if stop_after not in ("ffn1", "p1b"):
            st2 = phase2(st1b)

        def phase2c(att_stores, n_sub, after_wo=None):
            sb.mark()
            o_wo = sb.alloc(NDC * DM * 2)
            t_wo = Tile()
            pg.op("pool", lambda e: e.dma_start(out=A16(o_wo, [[DM, NDC], [1, DM]]),
                                                in_=DR(wout_d, 0, [[DM, P], [P * DM, NDC], [1, DM]])),
                  writes=[t_wo], lane=pg.lane())
            if after_wo is not None:
                after_wo()
            o_cat = [sb.alloc(DM * 2) for _ in range(2)]
            o_ob3 = [sb.alloc(3 * 520 * 4) for _ in range(2)]
            o_x = [sb.alloc(DM * 4) for _ in range(2)]
            o_cT = [sb.alloc(NDC * P * 2) for _ in range(2)]
            o_rd = sb.alloc(8 * 4)
            t_cat = [Tile(), Tile()]; l_cat = [pg.lane(), pg.lane()]
            t_ob3 = [Tile(), Tile()]; l_ob3 = [pg.lane(), pg.lane()]
            t_x = [Tile(), Tile()]; l_x = [pg.lane(), pg.lane()]; l_xs = [pg.lane(), pg.lane()]
            t_cT = [Tile(), Tile()]
            t_rd = Tile()
            bT = 7
            stores = []

            def stage_a(k):
                sl = k % 2
                pg.op("sp", lambda e: e.dma_start(out=A16(o_cat[sl], [[1, 512]]),
                                                  in_=DR(oa_d, k * P * 512, [[512, P], [1, 512]])),
                      writes=[t_cat[sl]], lane=l_cat[sl], extra_deps=att_stores)
                pg.op("sp", lambda e: e.dma_start(out=A32(o_ob3[sl], [[520, 3], [1, 520]]),
                                                  in_=DR(ob_d, k * P * 520, [[520, P], [OWN * 520, 3], [1, 520]])),
                      writes=[t_ob3[sl]], lane=l_ob3[sl], extra_deps=att_stores)
                pg.op("sp", lambda e: e.dma_start(out=A32(o_x[sl], [[1, DM]]),
                                                  in_=DR(x1_d, k * P * DM, [[DM, P], [1, DM]])),
                      writes=[t_x[sl]], lane=l_x[sl], extra_deps=att_stores)
                o0 = A32(o_ob3[sl], [[1, 520]])
                pg.op("dve", lambda e: e.tensor_tensor(out=o0, in0=o0, in1=A32(o_ob3[sl] + 520 * 4, [[1, 520]]), op=ALU.add),
                      reads=[t_ob3[sl]], writes=[t_ob3[sl]])
                pg.op("dve", lambda e: e.tensor_tensor(out=o0, in0=o0, in1=A32(o_ob3[sl] + 2 * 520 * 4, [[1, 520]]), op=ALU.add),
                      reads=[t_ob3[sl]], writes=[t_ob3[sl]])
                pg.op("dve", lambda e: e.reciprocal(out=A32(o_rd, [[1, 8]]), in_=A32(o_ob3[sl] + 64 * 4, [[65, 8]])),
                      reads=[t_ob3[sl]], writes=[t_rd])
                pg.op("dve", lambda e: e.tensor_tensor(out=A16(o_cat[sl] + 512 * 2, [[64, 8], [1, 64]]),
                                                       in0=A32(o_ob3[sl], [[65, 8], [1, 64]]),
                                                       in1=A32(o_rd, [[1, 8], [0, 64]]), op=ALU.mult),
                      reads=[t_ob3[sl], t_rd], writes=[t_cat[sl]])

            def stage_t(k):
                sl = k % 2

                def tr(e):
                    ins = None
                    for dc in range(NDC):
                        ins = e.transpose(out=PS16(bT, [[1, P]], off=dc * P),
                                          in_=A16(o_cat[sl] + dc * P * 2, [[1, P]]), identity=ident)
                    return ins
                pg.op("pe", tr, reads=[t_cat[sl], t_const], writes=[bankT[bT]])
                pg.op("act", lambda e: e.copy(out=A16(o_cT[sl], [[1, NDC * P]]), in_=PS16(bT, [[1, NDC * P]])),
                      reads=[bankT[bT]], writes=[t_cT[sl]])

            def stage_mm(k):
                sl = k % 2
                for hf in range(2):
                    bank = (2 * k + hf) % 4

                    def mm(e, hf=hf, bank=bank):
                        ins = None
                        for cc in range(NDC):
                            ins = e.matmul(out=PS32(bank, [[1, 512]]),
                                           lhsT=A16(o_cT[sl] + cc * P * 2, [[1, P]]),
                                           rhs=A16(o_wo + (cc * DM + hf * 512) * 2, [[1, 512]]),
                                           start=(cc == 0), stop=(cc == NDC - 1))
                        return ins
                    pg.op("pe", mm, reads=[t_cT[sl], t_wo], writes=[bankT[bank]])

            def stage_r(k):
                sl = k % 2
                for hf in range(2):
                    bank = (2 * k + hf) % 4
                    xh = A32(o_x[sl] + hf * 2048, [[1, 512]])
                    pg.op("dve", lambda e, bank=bank, xh=xh: e.tensor_tensor(out=xh, in0=PS32(bank, [[1, 512]]), in1=xh, op=ALU.add),
                          reads=[bankT[bank], t_x[sl]], writes=[t_x[sl]])
                dst = DR(x1_d, k * P * DM, [[DM, P], [1, DM]])
                stores.append(pg.op("sp", lambda e: e.dma_start(out=dst, in_=A32(o_x[sl], [[1, DM]])),
                                    reads=[t_x[sl]], lane=l_xs[sl]))

            stage_a(0)
            stage_t(0)
            for k in range(n_sub):
                if k + 1 < n_sub:
                    stage_a(k + 1)
                stage_mm(k)
                if k + 1 < n_sub:
                    stage_t(k + 1)
                stage_r(k)
            sb.release()
            pg.set_fence()
            return stores

        if stop_after == "all":
            sb.mark()
            pre2, load2 = alloc_load_ffn_weights((wg2_d, wu2_d, wd2_d), defer=True)
            st2c = phase2c(st2, OWN // P, after_wo=load2)
            ffn_pass("ffn2", OWN // TT, (wg2_d, wu2_d, wd2_d), 2, x1_d, out_d, final_gain_k=3, src_dep=st2c, pre=pre2)
            sb.release()

        finals = [lst[-1] for dom, lst in pg.dom_ops.items() if dom.startswith("L")]
        pg.op("sp", lambda e: e.nop(), extra_deps=finals)
        pg.emit(nc, stack)
    return nc


def _masks():
    a = np.arange(P)[:, None]
    b = np.arange(P)[None, :]
    m_prev = (b >= a)
    m = np.zeros((P, 4 * P), np.float32)
    m[:, 0:P] = (b <= a)
    m[:, P:2 * P] = (a <= b)
    m[:, 2 * P:3 * P] = (b <= a)
    m[:, 3 * P:4 * P] = (a <= b)
    return m


_CACHE = {}


def _get_nc(cfg_key, cfg):
    if cfg_key not in _CACHE:
        _CACHE[cfg_key] = build_program(cfg)
    return _CACHE[cfg_key]


def make_in_maps(inputs):
    x = np.asarray(inputs["x"], np.float32)
    pos = np.asarray(inputs["positions"], np.int32)
    gains = np.stack([np.asarray(inputs["norm_ffn1"], np.float32)[0], np.asarray(inputs["norm_mix"], np.float32)[0],
                      np.asarray(inputs["norm_ffn2"], np.float32)[0], np.asarray(inputs["norm_final"], np.float32)], 0)
    invf = (1.0 / (10000.0 ** (np.arange(0, 64, 2, dtype=np.float32) / 64.0))).astype(np.float32)
    invf = np.ascontiguousarray(np.broadcast_to(invf[None, :], (P, 32)))
    ident = np.eye(P, dtype=np.float32)
    masks = _masks()
    common = {
        "invf": invf, "ident": ident, "masks": masks, "gains": np.ascontiguousarray(gains),
        "a_sink": np.asarray(inputs["a_sink"], np.float32).reshape(1, 8),
        "w_gate1": np.ascontiguousarray(inputs["w_gate1"][0], dtype=np.float32),
        "w_up1": np.ascontiguousarray(inputs["w_up1"][0], dtype=np.float32),
        "w_down1": np.ascontiguousarray(inputs["w_down1"][0], dtype=np.float32),
        "w_gate2": np.ascontiguousarray(inputs["w_gate2"][0], dtype=np.float32),
        "w_up2": np.ascontiguousarray(inputs["w_up2"][0], dtype=np.float32),
        "w_down2": np.ascontiguousarray(inputs["w_down2"][0], dtype=np.float32),
        "w_in": np.ascontiguousarray(inputs["w_in"][0], dtype=np.float32),
        "w_out": np.ascontiguousarray(inputs["w_out"][0], dtype=np.float32),
    }
    maps = []
    for c in range(8):
        b, hf = c // 2, c % 2
        if hf == 0:
            xs = x[b, 0:LOC]
            ps = pos[b, 0:LOC]
        else:
            xs = x[b, SEQ - LOC:SEQ][::-1]
            ps = pos[b, SEQ - LOC:SEQ][::-1]
        m = dict(common)
        m["x"] = np.ascontiguousarray(xs)
        m["pos"] = np.ascontiguousarray(ps.reshape(LOC // P, P).T)
        maps.append(m)
    return maps


def kernel(**inputs):
    nc = _get_nc("full", {})
    maps = make_in_maps(inputs)
    res = run_bass_kernel_spmd(nc, maps, core_ids=list(range(8)))
    out = np.empty((BATCH, SEQ, DM), np.float32)
    for c in range(8):
        b, hf = c // 2, c % 2
        o = np.asarray(res.results[c]["out"], np.float32)
        if hf == 0:
            out[b, 0:OWN] = o
        else:
            out[b, SEQ - OWN:SEQ] = o[::-1]
    return out
```

```python
import contextlib
import numpy as np
import ml_dtypes
import concourse.bass as bass
import concourse.mybir as mybir
from concourse.bass_utils import run_bass_kernel_spmd

F32 = mybir.dt.float32
BF16 = mybir.dt.bfloat16
I32 = mybir.dt.int32
ALU = mybir.AluOpType
AF = mybir.ActivationFunctionType

P = 128
DM = 1024
DFF = 2816
NFC = 22
NDC = 8
SEQ = 8192
BATCH = 4
OWN = 4096
LOC = 5120
TT = 512
EPS = 1e-6
INW = 2304
C_AQ = 0
C_AK = 512
C_AV = 768
C_BQ = 900
C_BK = 1412
C_BV = 1924
ROWW = 2444

SB_BYTES = 206 * 1024
ROW32 = SB_BYTES // 4
ROW16 = SB_BYTES // 2


_FENCE = {}


class Tile:
    __slots__ = ("w", "r", "name")

    def __init__(self, name=""):
        self.w = {}
        self.r = dict(_FENCE)
        self.name = name


class Op:
    __slots__ = ("eng", "fn", "deps", "inc", "dom", "val", "is_dma", "idx")


COMPUTE = ("pe", "act", "dve", "pool")
ENGS = ("pe", "act", "dve", "pool", "sp")


class Prog:
    def __init__(self):
        self.ops = {e: [] for e in ENGS}
        self.dom_ops = {}
        self.nlanes = 0

    def lane(self):
        self.nlanes += 1
        return "L%d" % self.nlanes

    def op(self, eng, fn, reads=(), writes=(), lane=None, extra_deps=()):
        o = Op()
        o.eng = eng
        o.fn = fn
        o.is_dma = lane is not None
        o.dom = lane if lane is not None else eng
        o.inc = o.is_dma
        o.val = None
        deps = {}

        def add(p, kind):
            if p.dom == o.dom and not o.is_dma:
                if eng == "pe" or kind != "raw":
                    return
            q = deps.get(p.dom)
            if q is None or p.idx > q.idx:
                deps[p.dom] = p

        for t in reads:
            for p in t.w.values():
                add(p, "raw")
        for t in writes:
            for p in t.w.values():
                add(p, "waw")
            for p in t.r.values():
                add(p, "war")
        for p in extra_deps:
            add(p, "raw")
        o.deps = list(deps.values())
        for p in o.deps:
            p.inc = True
        lst = self.dom_ops.setdefault(o.dom, [])
        o.idx = len(lst)
        lst.append(o)
        self.ops[eng].append(o)
        for t in reads:
            t.r[o.dom] = o
        for t in writes:
            t.w[o.dom] = o
        return o

    def set_fence(self):
        _FENCE.clear()
        for dom, lst in self.dom_ops.items():
            _FENCE[dom] = lst[-1]

    def finalize(self):
        for dom, lst in self.dom_ops.items():
            c = 0
            for o in lst:
                if o.inc:
                    c += 16 if o.is_dma else 1
                    o.val = c

    def emit(self, nc, stack):
        self.finalize()
        sems = {}
        for dom in self.dom_ops:
            sems[dom] = stack.enter_context(nc.semaphore("s_" + dom))
        block = stack.enter_context(nc.Block())
        prog = self

        def run(eng_name):
            def body(e):
                seen = {}
                for o in prog.ops[eng_name]:
                    for d in o.deps:
                        if seen.get(d.dom, 0) >= d.val:
                            continue
                        e.wait_ge(sems[d.dom], d.val)
                        seen[d.dom] = d.val
                    ins = o.fn(e)
                    if o.inc:
                        ins.then_inc(sems[o.dom], 16 if o.is_dma else 1)
            return body

        block.tensor(run("pe"))
        block.scalar(run("act"))
        block.vector(run("dve"))
        block.gpsimd(run("pool"))
        block.sync(run("sp"))


class SbAlloc:
    def __init__(self, limit):
        self.off = 0
        self.limit = limit
        self.marks = []

    def alloc(self, nbytes):
        o = (self.off + 63) // 64 * 64
        self.off = o + nbytes
        assert self.off <= self.limit, "SBUF overflow %d" % self.off
        return o

    def mark(self):
        self.marks.append(self.off)

    def release(self):
        self.off = self.marks.pop()


def build_program(cfg):
    n_tiles1 = cfg.get("n_tiles1", LOC // TT)
    stop_after = cfg.get("stop_after", "all")
    nc = bass.Bass("TRN2", target_bir_lowering=False)
    dr = {}

    def din(name, shape, dt=F32):
        dr[name] = nc.dram_tensor(name, shape, dt, kind="ExternalInput")
        return dr[name]

    x_d = din("x", [LOC, DM])
    pos_d = din("pos", [P, LOC // P], I32)
    invf_d = din("invf", [P, 32])
    ident_d = din("ident", [P, P])
    mask_d = din("masks", [P, 4 * P])
    gains_d = din("gains", [4, DM])
    sink_d = din("a_sink", [1, 8])
    wg1_d = din("w_gate1", [DM, DFF]); wu1_d = din("w_up1", [DM, DFF]); wd1_d = din("w_down1", [DFF, DM])
    wg2_d = din("w_gate2", [DM, DFF]); wu2_d = din("w_up2", [DM, DFF]); wd2_d = din("w_down2", [DFF, DM])
    win_d = din("w_in", [DM, INW])
    wout_d = din("w_out", [DM, DM])
    out_d = nc.dram_tensor("out", [OWN, DM], F32, kind="ExternalOutput")
    dbg = cfg.get("debug", False)
    kind_s = "ExternalOutput" if dbg else "Internal"
    x1_d = nc.dram_tensor("x1s", [LOC, DM], F32, kind=kind_s)
    qkv_d = nc.dram_tensor("qkvs", [LOC, ROWW], BF16, kind=kind_s)
    oa_d = nc.dram_tensor("oas", [OWN, 512], BF16, kind=kind_s)
    ob_d = nc.dram_tensor("obs", [3, OWN, 520], F32, kind=kind_s)

    pg = Prog()
    _FENCE.clear()
    stack = contextlib.ExitStack()
    with stack:
        big32 = stack.enter_context(nc.sbuf_tensor("big", [P, ROW32], F32))
        big16 = big32.bitcast(BF16)
        bigi = big32.bitcast(I32)
        ps32 = stack.enter_context(nc.psum_tensor("ps", [P, 4096], F32))
        ps16 = ps32.bitcast(BF16)

        def A32(off, dims, p0=0, n=P):
            assert off % 4 == 0
            return bass.AP(big32, p0 * ROW32 + off // 4, [[ROW32, n]] + [list(d) for d in dims])

        def A16(off, dims, p0=0, n=P):
            assert off % 2 == 0
            return bass.AP(big16, p0 * ROW16 + off // 2, [[ROW16, n]] + [list(d) for d in dims])

        def AI(off, dims, p0=0, n=P):
            return bass.AP(bigi, p0 * ROW32 + off // 4, [[ROW32, n]] + [list(d) for d in dims])

        def PS32(bank, dims, off=0, p0=0, n=P):
            return bass.AP(ps32, p0 * 4096 + bank * 512 + off, [[4096, n]] + [list(d) for d in dims])

        def PS16(bank, dims, off=0, p0=0, n=P):
            return bass.AP(ps16, p0 * 8192 + bank * 1024 + off, [[8192, n]] + [list(d) for d in dims])

        def DR(t, off, dims):
            return bass.AP(t, off, [list(d) for d in dims])

        sb = SbAlloc(SB_BYTES)
        bankT = [Tile("bank%d" % i) for i in range(8)]

        o_ident = sb.alloc(P * 2)
        o_mask = sb.alloc(4 * P * 2)
        o_small = sb.alloc(1024)
        t_const = Tile("const")
        ident = A16(o_ident, [[1, P]])

        lc = pg.lane()
        pg.op("pool", lambda e: e.dma_start(out=A16(o_ident, [[1, P]]), in_=ident_d.ap()), writes=[t_const], lane=lc)
        lc2 = pg.lane()
        pg.op("pool", lambda e: e.dma_start(out=A16(o_mask, [[1, 4 * P]]), in_=mask_d.ap()), writes=[t_const], lane=lc2)
        def load_gain(k):
            o = sb.alloc(DM * 4)
            t = Tile("gain%d" % k)
            pg.op("sp", lambda e: e.dma_start(out=A32(o, [[1, DM]]), in_=DR(gains_d, k * DM, [[0, P], [1, DM]])),
                  writes=[t], lane=pg.lane())
            return A32(o, [[1, DM]]), t

        small_ctr = [0]

        def small_col():
            c = small_ctr[0] % 256
            small_ctr[0] += 1
            return o_small + 4 * c

        def load_weights_ffn(wg_d, wu_d, wd_d, o_wg, o_wu, o_wd, tiles, parts=("gu", "d"), d_deps=()):
            for g in (range(4) if "gu" in parts else ()):
                for (w_d, o_w, key) in ((wg_d, o_wg, "g"), (wu_d, o_wu, "u")):
                    src = DR(w_d, g * 704, [[DFF, P], [P * DFF, NDC], [1, 704]])
                    dst = A16(o_w + g * 704 * 2, [[DFF, NDC], [1, 704]])
                    pg.op("pool", (lambda e, s=src, d=dst: e.dma_start(out=d, in_=s)),
                          writes=[tiles[key][g]], lane=pg.lane())
            for hf in (range(2) if "d" in parts else ()):
                src = DR(wd_d, hf * 11 * P * DM, [[DM, P], [P * DM, 11], [1, DM]])
                dst = A16(o_wd + hf * 11 * DM * 2, [[DM, 11], [1, DM]])
                pg.op("pool", (lambda e, s=src, d=dst: e.dma_start(out=d, in_=s)),
                      writes=[tiles["d"][hf]], lane=pg.lane(), extra_deps=d_deps)

        def rms_stats(x_ap, t_x, t_stat, col, junk_ap, t_junk):
            ss = A32(col, [[1, 1]])
            pg.op("act", lambda e: e.activation(out=junk_ap, in_=x_ap, func=AF.Square,
                                                scale=1.0 / 32.0, accum_out=ss),
                  reads=[t_x], writes=[t_stat, t_junk])
            pg.op("act", lambda e: e.activation(out=ss, in_=ss, func=AF.Sqrt, bias=EPS, scale=1.0),
                  reads=[t_stat], writes=[t_stat])
            pg.op("dve", lambda e: e.reciprocal(out=ss, in_=ss), reads=[t_stat], writes=[t_stat])
            return ss

        def alloc_load_ffn_weights(wdr, defer=False):
            o_wg = sb.alloc(NDC * DFF * 2)
            o_wu = sb.alloc(NDC * DFF * 2)
            o_wd = sb.alloc(NFC * DM * 2)
            wt = {"g": [Tile() for _ in range(4)], "u": [Tile() for _ in range(4)], "d": [Tile() for _ in range(2)]}
            if defer:
                return (o_wg, o_wu, o_wd, wt), (lambda parts=("gu", "d"), d_deps=(): load_weights_ffn(
                    wdr[0], wdr[1], wdr[2], o_wg, o_wu, o_wd, wt, parts=parts, d_deps=d_deps))
            load_weights_ffn(wdr[0], wdr[1], wdr[2], o_wg, o_wu, o_wd, wt)
            return o_wg, o_wu, o_wd, wt

        def ffn_pass(name, n_tiles, wdr, gain_k, src_d, dst_d, final_gain_k=None, src_dep=None, pre=None, tail_hook=None):
            sb.mark()
            if pre is None:
                o_wg, o_wu, o_wd, wt = alloc_load_ffn_weights(wdr)
            else:
                o_wg, o_wu, o_wd, wt = pre
            o_xn = [sb.alloc(DM * 4) for _ in range(2)]
            o_xr = [sb.alloc(DM * 4) for _ in range(2)]
            o_h = [sb.alloc(DM * 2) for _ in range(2)]
            o_hT = [sb.alloc(NDC * TT * 2) for _ in range(2)]
            o_act = sb.alloc(NFC * TT * 2)
            o_sg = [sb.alloc(TT * 4)] * 2
            o_junk2 = sb.alloc(DM * 2)
            t_junk2 = Tile()
            gain_a, t_gain = load_gain(gain_k)
            if final_gain_k is not None:
                fgain_a, t_fgain = load_gain(final_gain_k)
            t_xn = [Tile() for _ in range(2)]; l_xn = [pg.lane() for _ in range(2)]
            t_xr = [Tile() for _ in range(2)]; l_xr = [pg.lane() for _ in range(2)]
            l_st = [pg.lane() for _ in range(2)]
            t_h = [Tile() for _ in range(2)]
            t_hT = [[Tile() for _ in range(4)] for _ in range(2)]
            t_act = [Tile() for _ in range(NFC)]
            t_sg = [Tile()] * 2
            t_stat = Tile()
            bG = [0, 1]; bU = [2, 3]; bD = [4, 5, 6]; bT = 7
            cnt = {"xn": 0, "xr": 0, "d": 0}

            def norm_sub(i, s, part):
                k = i * 4 + s
                sl = k % 2
                hb = i % 2
                if part == 0:
                    src = DR(src_d, (i * TT + s * P) * DM, [[DM, P], [1, DM]])
                    xa = A32(o_xn[sl], [[1, DM]])
                    pg.op("sp", lambda e: e.dma_start(out=xa, in_=src), writes=[t_xn[sl]], lane=l_xn[sl],
                          extra_deps=([src_dep[k]] if src_dep else ()))
                    col = small_col()
                    ha = A16(o_h[sl], [[1, DM]])
                    ss = rms_stats(xa, t_xn[sl], t_stat, col, ha, t_h[sl])
                    pg.op("dve", lambda e: e.scalar_tensor_tensor(out=ha, in0=xa, scalar=ss, in1=gain_a,
                                                                  op0=ALU.mult, op1=ALU.mult),
                          reads=[t_xn[sl], t_stat, t_gain], writes=[t_h[sl]])
                else:
                    def tr(e):
                        ins = None
                        for dc in range(NDC):
                            ins = e.transpose(out=PS16(bT, [[1, P]], off=dc * P),
                                              in_=A16(o_h[sl] + dc * P * 2, [[1, P]]), identity=ident)
                        return ins
                    pg.op("pe", tr, reads=[t_h[sl], t_const], writes=[bankT[bT]])
                    dst = A16(o_hT[hb] + s * P * 2, [[TT, NDC], [1, P]])
                    pg.op("act", lambda e: e.copy(out=dst, in_=PS16(bT, [[P, NDC], [1, P]])),
                          reads=[bankT[bT]], writes=[t_hT[hb][s]])

            def gate_up(i, f):
                hb = i % 2
                g = f * P // 704
                g2 = (f * P + P - 1) // 704
                wtiles = list({wt["g"][g], wt["g"][g2], wt["u"][g], wt["u"][g2]})

                def mm(e, o_w, bank):
                    ins = None
                    for dc in range(NDC):
                        ins = e.matmul(out=PS32(bank, [[1, TT]]),
                                       lhsT=A16(o_w + (dc * DFF + f * P) * 2, [[1, P]]),
                                       rhs=A16(o_hT[hb] + dc * TT * 2, [[1, TT]]),
                                       start=(dc == 0), stop=(dc == NDC - 1))
                    return ins
                bg = bG[f % 2]; bu = bU[f % 2]
                pg.op("pe", lambda e: mm(e, o_wg, bg), reads=t_hT[hb] + wtiles, writes=[bankT[bg]])
                pg.op("pe", lambda e: mm(e, o_wu, bu), reads=t_hT[hb] + wtiles, writes=[bankT[bu]])
                sg = A32(o_sg[f % 2], [[1, TT]])
                pg.op("act", lambda e: e.activation(out=sg, in_=PS32(bg, [[1, TT]]), func=AF.Silu),
                      reads=[bankT[bg]], writes=[t_sg[f % 2]])
                pg.op("dve", lambda e: e.tensor_tensor(out=A16(o_act + f * TT * 2, [[1, TT]]), in0=sg,
                                                       in1=PS32(bu, [[1, TT]]), op=ALU.mult),
                      reads=[t_sg[f % 2], bankT[bu]], writes=[t_act[f]])

            def down_sub(i, s):
                k = i * 4 + s
                sl = k % 2
                row0 = i * TT + s * P
                src = DR(src_d, row0 * DM, [[DM, P], [1, DM]])
                xr = A32(o_xr[sl], [[1, DM]])
                pg.op("sp", lambda e: e.dma_start(out=xr, in_=src), writes=[t_xr[sl]], lane=l_xr[sl],
                      extra_deps=([src_dep[k]] if src_dep else ()))
                for hf in range(2):
                    bank = bD[cnt["d"] % 3]
                    cnt["d"] += 1

                    def mm(e, bank=bank, hf=hf):
                        ins = None
                        for f in range(NFC):
                            ins = e.matmul(out=PS32(bank, [[1, 512]]),
                                           lhsT=A16(o_act + (f * TT + s * P) * 2, [[1, P]]),
                                           rhs=A16(o_wd + (f * DM + hf * 512) * 2, [[1, 512]]),
                                           start=(f == 0), stop=(f == NFC - 1))
                        return ins
                    pg.op("pe", mm, reads=t_act + wt["d"], writes=[bankT[bank]])
                    xh = A32(o_xr[sl] + hf * 2048, [[1, 512]])
                    pg.op("dve", lambda e, bank=bank, xh=xh: e.scalar_tensor_tensor(
                        out=xh, in0=PS32(bank, [[1, 512]]), scalar=0.5, in1=xh, op0=ALU.mult, op1=ALU.add),
                        reads=[bankT[bank], t_xr[sl]], writes=[t_xr[sl]])
                if final_gain_k is not None:
                    col = small_col()
                    ss = rms_stats(xr, t_xr[sl], t_stat, col, A16(o_junk2, [[1, DM]]), t_junk2)
                    pg.op("dve", lambda e: e.scalar_tensor_tensor(out=xr, in0=xr, scalar=ss, in1=fgain_a,
                                                                  op0=ALU.mult, op1=ALU.mult),
                          reads=[t_xr[sl], t_stat, t_fgain], writes=[t_xr[sl]])
                dst = DR(dst_d, row0 * DM, [[DM, P], [1, DM]])
                return pg.op("pool", lambda e: e.dma_start(out=dst, in_=xr), reads=[t_xr[sl]], lane=l_st[sl])

            stores = []
            for s in range(4):
                norm_sub(0, s, 0)
                norm_sub(0, s, 1)
            for i in range(n_tiles):
                for f in range(NFC):
                    gate_up(i, f)
                    if i + 1 < n_tiles:
                        if f in (1, 6, 11, 16):
                            norm_sub(i + 1, (f - 1) // 5, 0)
                        if f in (5, 10, 15, 20):
                            norm_sub(i + 1, (f - 5) // 5, 1)
                if tail_hook is not None and i == n_tiles - 1:
                    tail_hook(o_wg, pg.ops["pe"][-1])
                for s in range(4):
                    stores.append(down_sub(i, s))
            sb.release()
            pg.set_fence()
            return stores

        win_pre = {}

        def load_win(o_win, extra):
            t_win = [Tile(), Tile()]
            for hf in range(2):
                src = DR(win_d, hf * 1152, [[INW, P], [P * INW, NDC], [1, 1152]])
                dst = A16(o_win + hf * 1152 * 2, [[INW, NDC], [1, 1152]])
                pg.op("pool", (lambda e, s=src, d=dst: e.dma_start(out=d, in_=s)), writes=[t_win[hf]], lane=pg.lane(),
                      extra_deps=extra)
            return t_win

        def win_hook(o_wg, last_pe_op):
            win_pre["o"] = o_wg
            win_pre["t"] = load_win(o_wg, [last_pe_op])

        st1 = ffn_pass("ffn1", n_tiles1, (wg1_d, wu1_d, wd1_d), 0, x_d, x1_d,
                       tail_hook=(win_hook if cfg.get("win_prefetch", True) and stop_after != "ffn1" else None))
        n_sub1 = n_tiles1 * 4

        def phase1b(n_sub, x1_stores):
            import math
            sb.mark()
            o_win = sb.alloc(NDC * INW * 2)
            if "o" in win_pre:
                assert win_pre["o"] == o_win, (win_pre["o"], o_win)
                t_win = win_pre["t"]
            else:
                t_win = load_win(o_win, [])
            gain_a, t_gain = load_gain(1)
            NS = LOC // P
            o_pos = sb.alloc(NS * 4); o_posf = sb.alloc(NS * 4); o_invf = sb.alloc(32 * 4)
            o_ang = sb.alloc(NS * 32 * 4); o_u = sb.alloc(NS * 32 * 4)
            o_cos = sb.alloc(NS * 64 * 4); o_sin = sb.alloc(NS * 64 * 4)
            t_rope = Tile()
            pg.op("sp", lambda e: e.dma_start(out=AI(o_pos, [[1, NS]]), in_=pos_d.ap()), writes=[t_rope], lane=pg.lane())
            pg.op("sp", lambda e: e.dma_start(out=A32(o_invf, [[1, 32]]), in_=invf_d.ap()), writes=[t_rope], lane=pg.lane())
            pg.op("dve", lambda e: e.tensor_copy(A32(o_posf, [[1, NS]]), AI(o_pos, [[1, NS]])), reads=[t_rope], writes=[t_rope])
            pg.op("dve", lambda e: e.tensor_tensor(out=A32(o_ang, [[32, NS], [1, 32]]), in0=A32(o_posf, [[1, NS], [0, 32]]),
                                                   in1=A32(o_invf, [[0, NS], [1, 32]]), op=ALU.mult),
                  reads=[t_rope], writes=[t_rope])
            C1 = 6.28125
            C2 = 2 * math.pi - C1
            o_v = sb.alloc(NS * 32 * 4); o_ki = sb.alloc(NS * 32 * 4); o_kf = sb.alloc(NS * 32 * 4); o_m = sb.alloc(NS * 32 * 4)
            NE = NS * 32

            def reduce_angle(shift):
                fl = lambda o: A32(o, [[1, NE]])
                pg.op("dve", lambda e: e.tensor_scalar(out=fl(o_v), in0=fl(o_ang), scalar1=1.0 / (2 * math.pi),
                                                       scalar2=0.5 + shift / (2 * math.pi), op0=ALU.mult, op1=ALU.add),
                      reads=[t_rope], writes=[t_rope])
                pg.op("dve", lambda e: e.tensor_copy(AI(o_ki, [[1, NE]]), fl(o_v)), reads=[t_rope], writes=[t_rope])
                pg.op("dve", lambda e: e.tensor_copy(fl(o_kf), AI(o_ki, [[1, NE]])), reads=[t_rope], writes=[t_rope])
                pg.op("dve", lambda e: e.scalar_tensor_tensor(out=fl(o_u), in0=fl(o_kf), scalar=-C1, in1=fl(o_ang),
                                                              op0=ALU.mult, op1=ALU.add), reads=[t_rope], writes=[t_rope])
                pg.op("dve", lambda e: e.scalar_tensor_tensor(out=fl(o_u), in0=fl(o_kf), scalar=-C2, in1=fl(o_u),
                                                              op0=ALU.mult, op1=ALU.add), reads=[t_rope], writes=[t_rope])
                pg.op("dve", lambda e: e.tensor_scalar(out=fl(o_m), in0=fl(o_u), scalar1=-math.pi - shift, scalar2=None,
                                                       op0=ALU.is_lt), reads=[t_rope], writes=[t_rope])
                pg.op("dve", lambda e: e.scalar_tensor_tensor(out=fl(o_u), in0=fl(o_m), scalar=2 * math.pi, in1=fl(o_u),
                                                              op0=ALU.mult, op1=ALU.add), reads=[t_rope], writes=[t_rope])

            reduce_angle(0.0)
            pg.op("act", lambda e: e.activation(out=A32(o_sin + 32 * 4, [[64, NS], [1, 32]]), in_=A32(o_u, [[32, NS], [1, 32]]),
                                                func=AF.Sin, bias=0.0, scale=1.0), reads=[t_rope], writes=[t_rope])
            pg.op("act", lambda e: e.activation(out=A32(o_sin, [[64, NS], [1, 32]]), in_=A32(o_u, [[32, NS], [1, 32]]),
                                                func=AF.Sin, bias=0.0, scale=-1.0), reads=[t_rope], writes=[t_rope])
            reduce_angle(math.pi / 2)
            pg.op("act", lambda e: e.activation(out=A32(o_cos, [[64, NS], [1, 32]]), in_=A32(o_u, [[32, NS], [1, 32]]),
                                                func=AF.Sin, bias=math.pi / 2, scale=1.0), reads=[t_rope], writes=[t_rope])
            pg.op("act", lambda e: e.activation(out=A32(o_cos + 32 * 4, [[64, NS], [1, 32]]), in_=A32(o_u, [[32, NS], [1, 32]]),
                                                func=AF.Sin, bias=math.pi / 2, scale=1.0), reads=[t_rope], writes=[t_rope])

            o_xa = [sb.alloc(DM * 4) for _ in range(2)]
            o_hb = [sb.alloc(DM * 2) for _ in range(2)]
            o_hT2 = [sb.alloc(NDC * P * 2) for _ in range(2)]
            o_tA = [sb.alloc(512 * 4) for _ in range(2)]
            o_tB = [sb.alloc(512 * 4) for _ in range(2)]
            o_row = [sb.alloc(ROWW * 2) for _ in range(2)]
            t_xa = [Tile(), Tile()]; l_xa = [pg.lane(), pg.lane()]
            t_hb = [Tile(), Tile()]; t_hT2 = [Tile(), Tile()]
            t_tA = [Tile(), Tile()]; t_tB = [Tile(), Tile()]
            t_row = [Tile(), Tile()]; l_row = [pg.lane(), pg.lane()]
            t_stat = Tile()
            for sl in range(2):
                pg.op("pool", lambda e, sl=sl: e.memset(A16(o_row[sl] + (C_AV + 64) * 2, [[65, 2], [1, 1]]), 1.0), writes=[t_row[sl]])
                pg.op("pool", lambda e, sl=sl: e.memset(A16(o_row[sl] + (C_BV + 64) * 2, [[65, 8], [1, 1]]), 1.0), writes=[t_row[sl]])
            bT = 7
            grp = [(0, 512), (512, 512), (1024, 512), (1536, 512), (2048, 256)]
            cntj = [0]
            stores = []
            def stage_a(k):
                sl = k % 2
                src = DR(x1_d, k * P * DM, [[DM, P], [1, DM]])
                xa = A32(o_xa[sl], [[1, DM]])
                pg.op("sp", lambda e: e.dma_start(out=xa, in_=src), writes=[t_xa[sl]], lane=l_xa[sl],
                      extra_deps=[x1_stores[k]])
                ha = A16(o_hb[sl], [[1, DM]])
                ss = rms_stats(xa, t_xa[sl], t_stat, small_col(), ha, t_hb[sl])
                pg.op("dve", lambda e: e.scalar_tensor_tensor(out=ha, in0=xa, scalar=ss, in1=gain_a,
                                                              op0=ALU.mult, op1=ALU.mult),
                      reads=[t_xa[sl], t_stat, t_gain], writes=[t_hb[sl]])

            def stage_t(k):
                sl = k % 2

                def tr(e):
                    ins = None
                    for dc in range(NDC):
                        ins = e.transpose(out=PS16(bT, [[1, P]], off=dc * P),
                                          in_=A16(o_hb[sl] + dc * P * 2, [[1, P]]), identity=ident)
                    return ins
                pg.op("pe", tr, reads=[t_hb[sl], t_const], writes=[bankT[bT]])
                pg.op("act", lambda e: e.copy(out=A16(o_hT2[sl], [[1, NDC * P]]), in_=PS16(bT, [[1, NDC * P]])),
                      reads=[bankT[bT]], writes=[t_hT2[sl]])

            def stage_mm(k, gis):
                sl = k % 2
                for gi in gis:
                    c0, cw = grp[gi]

                    def mm(e, gi=gi, c0=c0, cw=cw):
                        ins = None
                        for dc in range(NDC):
                            ins = e.matmul(out=PS32(gi, [[1, cw]]),
                                           lhsT=A16(o_hT2[sl] + dc * P * 2, [[1, P]]),
                                           rhs=A16(o_win + (dc * INW + c0) * 2, [[1, cw]]),
                                           start=(dc == 0), stop=(dc == NDC - 1))
                        return ins
                    pg.op("pe", mm, reads=[t_hT2[sl]] + t_win, writes=[bankT[gi]])

            def stage_c(k):
                sl = k % 2
                cosk = o_cos + k * 64 * 4
                sink = o_sin + k * 64 * 4

                def rope(bank, off, H, dst_c, dup=False):
                    j = cntj[0] % 2
                    cntj[0] += 1
                    tA = A32(o_tA[j], [[64, H], [1, 64]])
                    pg.op("dve", lambda e: e.tensor_tensor(out=tA, in0=PS32(bank, [[64, H], [1, 64]], off=off),
                                                           in1=A32(cosk, [[0, H], [1, 64]]), op=ALU.mult),
                          reads=[bankT[bank], t_rope], writes=[t_tA[j]])
                    pg.op("dve", lambda e: e.tensor_tensor(out=A32(o_tB[j], [[64, H], [1, 32]]),
                                                           in0=PS32(bank, [[64, H], [1, 32]], off=off + 32),
                                                           in1=A32(sink, [[0, H], [1, 32]]), op=ALU.mult),
                          reads=[bankT[bank], t_rope], writes=[t_tB[j]])
                    pg.op("dve", lambda e: e.tensor_tensor(out=A32(o_tB[j] + 32 * 4, [[64, H], [1, 32]]),
                                                           in0=PS32(bank, [[64, H], [1, 32]], off=off),
                                                           in1=A32(sink + 32 * 4, [[0, H], [1, 32]]), op=ALU.mult),
                          reads=[bankT[bank], t_rope], writes=[t_tB[j]])
                    if not dup:
                        pg.op("pool", lambda e: e.tensor_tensor(out=A16(o_row[sl] + dst_c * 2, [[64, H], [1, 64]]), in0=tA,
                                                                in1=A32(o_tB[j], [[64, H], [1, 64]]), op=ALU.add),
                              reads=[t_tA[j], t_tB[j]], writes=[t_row[sl]])
                    else:
                        for du in range(2):
                            pg.op("pool", lambda e, du=du: e.tensor_tensor(
                                out=A16(o_row[sl] + (C_AK + du * 64) * 2, [[128, H], [1, 64]]), in0=tA,
                                in1=A32(o_tB[j], [[64, H], [1, 64]]), op=ALU.add),
                                reads=[t_tA[j], t_tB[j]], writes=[t_row[sl]])

                rope(0, 0, 8, C_AQ)
                rope(1, 0, 2, None, dup=True)
                rope(1, 256, 4, C_BQ)
                pg.op("act", lambda e: e.copy(out=A16(o_row[sl] + C_AV * 2, [[65, 2], [1, 64]]),
                                              in_=PS32(1, [[64, 2], [1, 64]], off=128)),
                      reads=[bankT[1]], writes=[t_row[sl]])
                rope(2, 0, 8, C_BQ + 256)
                rope(3, 0, 4, C_BK + 256)
                pg.op("act", lambda e: e.copy(out=A16(o_row[sl] + C_BV * 2, [[65, 4], [1, 64]]),
                                              in_=PS32(3, [[64, 4], [1, 64]], off=256)),
                      reads=[bankT[3]], writes=[t_row[sl]])
                pg.op("act", lambda e: e.copy(out=A16(o_row[sl] + (C_BV + 4 * 65) * 2, [[65, 4], [1, 64]]),
                                              in_=PS32(4, [[64, 4], [1, 64]], off=0)),
                      reads=[bankT[4]], writes=[t_row[sl]])
                dst = DR(qkv_d, k * P * ROWW, [[ROWW, P], [1, ROWW]])
                stores.append(pg.op("pool", lambda e: e.dma_start(out=dst, in_=A16(o_row[sl], [[1, ROWW]])),
                                    reads=[t_row[sl]], lane=l_row[sl]))

            stage_a(0)
            stage_t(0)
            for k in range(n_sub):
                if k + 1 < n_sub:
                    stage_a(k + 1)
                stage_mm(k, [0, 1, 2])
                if k + 1 < n_sub:
                    stage_t(k + 1)
                stage_mm(k, [3, 4])
                stage_c(k)
            sb.release()
            pg.set_fence()
            return stores

        st1b = []
        if stop_after != "ffn1":
            st1b = phase1b(n_sub1, st1)

        def phase2(qkv_stores):
            sb.mark()
            NKB = 12
            NQB = 12
            RAWK = 1032
            QTW = 4 * NQB * P
            o_es = sb.alloc(8 * 4)
            t_es = Tile()
            pg.op("sp", lambda e: e.dma_start(out=A32(o_es, [[1, 8]]), in_=DR(sink_d, 0, [[0, P], [1, 8]])),
                  writes=[t_es], lane=pg.lane())
            pg.op("act", lambda e: e.activation(out=A32(o_es, [[1, 8]]), in_=A32(o_es, [[1, 8]]), func=AF.Exp),
                  reads=[t_es], writes=[t_es])
            o_kv = [sb.alloc(NKB * RAWK * 2) for _ in range(2)]
            o_q = [sb.alloc(NQB * 512 * 2) for _ in range(2)]
            o_kT = [sb.alloc(4 * NKB * P * 2) for _ in range(2)]
            o_qT = [sb.alloc(2 * QTW * 2) for _ in range(2)]
            NPT = cfg.get("npt", 6)
            o_pT = [sb.alloc(6 * P * 2) for _ in range(NPT)]
            NOB = 4
            o_ob = [sb.alloc(520 * 4) for _ in range(NOB)]
            o_ca = [sb.alloc(512 * 2) for _ in range(NOB)]
            o_den = sb.alloc(16 * 4)
            t_kv = [Tile(), Tile()]; t_q = [Tile(), Tile()]
            l_kv = [[pg.lane(), pg.lane()] for _ in range(2)]; l_q = [[pg.lane(), pg.lane()] for _ in range(2)]
            t_kT = [Tile(), Tile()]; t_qT = [Tile(), Tile()]
            t_pT = [Tile() for _ in range(NPT)]
            t_ob = [Tile() for _ in range(NOB)]; l_ob = [pg.lane() for _ in range(NOB)]
            t_ca = [Tile() for _ in range(NOB)]; l_ca = [pg.lane() for _ in range(NOB)]
            t_den = Tile()
            t_fill = Tile()
            NFILL = cfg.get("fill", 0)
            bS = [(0, 1), (2, 3)]; bO = [(4, 5), (6, 4), (5, 6)]; bT = 7
            ctr = {"S": 0, "pT": 0, "ob": 0, "ca": 0, "m": 0, "qb": 0, "T": 0}
            bTs = (7, 3)
            out_stores = []
            for b in range(2):
                pg.op("pool", lambda e, b=b: e.memset(A16(o_kv[b], [[1, NKB * RAWK]]), 0.0), writes=[t_kv[b]])
                pg.op("pool", lambda e, b=b: e.memset(A16(o_q[b], [[1, NQB * 512]]), 0.0), writes=[t_q[b]])
                pg.op("pool", lambda e, b=b: e.memset(A16(o_kT[b], [[1, 4 * NKB * P]]), 0.0), writes=[t_kT[b]])
                pg.op("pool", lambda e, b=b: e.memset(A16(o_qT[b], [[1, 2 * QTW]]), 0.0), writes=[t_qT[b]])

            segs = []
            for s_ in range(4):
                segs.append([("A", 1, 0, 8 * s_, 8 * s_ + 8, None)])
            for (n0, n1) in ((0, 9), (9, 18), (18, 27), (27, 33)):
                segs.append([("B", 1, 0, n0, n1, 0)])
            for r in range(4):
                segs.append([("B", 4, r, 0, 9, 1)])
            for r in range(0, 16, 4):
                segs.append([("B", 16, r + i_, 0, 3, 2) for i_ in range(4)])

            def seg_geom(seg):
                kind, D, r, n0, n1, pat = seg
                Lk = LOC // D
                Lq = OWN // D
                if kind == "A":
                    j0 = max(0, n0 - 1); j1 = n1
                    q_lo = P * n0
                else:
                    j0 = max(0, n0 - 1); j1 = n1 - 1
                    q_lo = P * n0 - 64
                j1 = min(j1, (Lk - 1) // P)
                return Lk, Lq, j0, j1, q_lo

            def part_prepare(seg, buf, kb_base, qb_base):
                kind, D, r, n0, n1, pat = seg
                Lk, Lq, j0, j1, q_lo = seg_geom(seg)
                if kind == "A":
                    kc0, kcols, nkc, qc0 = C_AK, 256 + 130, 2, C_AQ
                else:
                    kc0, kcols, nkc, qc0 = C_BK, 512 + 520, 4, C_BQ
                k_lo = P * j0
                k_hi = min(Lk, P * (j1 + 1))
                nfull = (k_hi - k_lo) // P
                krem = (k_hi - k_lo) % P
                tok0 = r + D * k_lo
                if nfull:
                    src = DR(qkv_d, tok0 * ROWW + kc0, [[D * ROWW, P], [P * D * ROWW, nfull], [1, kcols]])
                    dst = A16(o_kv[buf] + kb_base * RAWK * 2, [[RAWK, nfull], [1, kcols]])
                    pg.op("sp", lambda e, dst=dst, src=src: e.dma_start(out=dst, in_=src), writes=[t_kv[buf]],
                          lane=l_kv[buf][0], extra_deps=qkv_stores)
                if krem:
                    src2 = DR(qkv_d, (tok0 + D * P * nfull) * ROWW + kc0, [[D * ROWW, krem], [1, kcols]])
                    dst2 = A16(o_kv[buf] + (kb_base + nfull) * RAWK * 2, [[1, kcols]], n=krem)
                    pg.op("sp", lambda e, dst2=dst2, src2=src2: e.dma_start(out=dst2, in_=src2), writes=[t_kv[buf]],
                          lane=l_kv[buf][1], extra_deps=qkv_stores)
                nkb = nfull + (1 if krem else 0)
                nqb = n1 - n0
                jq0 = 0
                if q_lo < 0:
                    src2 = DR(qkv_d, r * ROWW + qc0, [[D * ROWW, 64], [1, 512]])
                    dst2 = A16(o_q[buf] + qb_base * 512 * 2, [[1, 512]], p0=64, n=64)
                    pg.op("sp", lambda e, dst2=dst2, src2=src2: e.dma_start(out=dst2, in_=src2), writes=[t_q[buf]],
                          lane=l_q[buf][1], extra_deps=qkv_stores)
                    jq0 = 1
                if nqb - jq0 > 0:
                    qtok0 = r + D * (q_lo + P * jq0)
                    src = DR(qkv_d, qtok0 * ROWW + qc0, [[D * ROWW, P], [P * D * ROWW, nqb - jq0], [1, 512]])
                    dst = A16(o_q[buf] + (qb_base + jq0) * 512 * 2, [[512, nqb - jq0], [1, 512]])
                    pg.op("sp", lambda e, dst=dst, src=src: e.dma_start(out=dst, in_=src), writes=[t_q[buf]],
                          lane=l_q[buf][0], extra_deps=qkv_stores)
                yield
                for jj in range(nkb):
                    bT = bTs[ctr["T"] % 2]; ctr["T"] += 1

                    def trk(e, jj=jj, bT=bT):
                        ins = None
                        for c in range(nkc):
                            ins = e.transpose(out=PS16(bT, [[1, P]], off=c * P),
                                              in_=A16(o_kv[buf] + ((kb_base + jj) * RAWK + c * P) * 2, [[1, P]]), identity=ident)
                        return ins
                    pg.op("pe", trk, reads=[t_kv[buf], t_const], writes=[bankT[bT]])
                    if cfg.get("kevac_act", True):
                        pg.op("act", lambda e, jj=jj, bT=bT: e.copy(
                            out=A16(o_kT[buf] + (kb_base + jj) * P * 2, [[NKB * P, nkc], [1, P]]), in_=PS16(bT, [[P, nkc], [1, P]])),
                            reads=[bankT[bT]], writes=[t_kT[buf]])
                    else:
                        pg.op("dve", lambda e, jj=jj, bT=bT: e.tensor_copy(
                            A16(o_kT[buf] + (kb_base + jj) * P * 2, [[NKB * P, nkc], [1, P]]), PS16(bT, [[P, nkc], [1, P]])),
                            reads=[bankT[bT]], writes=[t_kT[buf]])
                    yield
                for jj in range(nqb):
                    bT = bTs[ctr["T"] % 2]; ctr["T"] += 1

                    def trq(e, jj=jj, bT=bT):
                        ins = None
                        for c in range(4):
                            ins = e.transpose(out=PS16(bT, [[1, P]], off=c * P),
                                              in_=A16(o_q[buf] + ((qb_base + jj) * 512 + c * P) * 2, [[1, P]]), identity=ident)
                        return ins
                    pg.op("pe", trq, reads=[t_q[buf], t_const], writes=[bankT[bT]])
                    for e_ in range(2):
                        pg.op("dve", lambda e, jj=jj, e_=e_, bT=bT: e.tensor_copy(
                            A16(o_qT[buf] + (e_ * QTW + (qb_base + jj) * P) * 2, [[NQB * P, 4], [1, P]], p0=64 * e_, n=64),
                            PS16(bT, [[P, 4], [1, P]], p0=64 * e_, n=64)),
                            reads=[bankT[bT]], writes=[t_qT[buf]])
                    yield

            def part_units(seg, buf, kb_base, qb_base):
                kind, D, r, n0, n1, pat = seg
                Lk, Lq, j0, j1, q_lo = seg_geom(seg)
                for n in range(n0, n1):
                    if kind == "A":
                        qs = P * n
                        blocks = [(jb, mk) for jb, mk in ((n - 1, 0), (n, None), (n + 1, 1)) if jb >= 0]
                        v0, v1 = 0, P
                    else:
                        qs = P * n - 64
                        blocks = [(jb, mk) for jb, mk in ((n - 1, 2), (n, 3)) if jb >= 0 and P * jb < Lk]
                        v0 = 64 if qs < 0 else 0
                        v1 = 64 if qs + P > Lq else P
                    qcol = qs - q_lo + qb_base * P
                    bOs = bO[ctr["qb"] % 3]
                    ctr["qb"] += 1
                    for c in range(4):
                        yield unit(kind, D, r, pat, buf, j0 - kb_base, blocks, qcol, qs, c, bOs, v0, v1)

            def unit(kind, D, r, pat, buf, j0, blocks, qcol, qs, c, bOs, v0, v1):
                nb = len(blocks)
                kc = (c // 2) if kind == "A" else c
                st = {}

                def front():
                    if nb > 2:
                        bss = ((0, 1), (1, 2))[ctr["S"] % 2]
                    else:
                        b1 = ctr["S"] % 3
                        bss = (b1, b1)
                    ctr["S"] += 1
                    pi = ctr["pT"] % NPT; ctr["pT"] += 1
                    st["pi"] = pi
                    sb0 = bss[0]

                    def score(e):
                        ins = None
                        for i, (jb, mk) in enumerate(blocks):
                            ins = e.matmul(out=PS32(sb0, [[1, 2 * P]], off=i * 2 * P),
                                           lhsT=A16(o_kT[buf] + (kc * NKB * P + (jb - j0) * P) * 2, [[1, P]]),
                                           rhs=A16(o_qT[buf] + (c * NQB * P + qcol) * 2, [[QTW, 2], [1, P]]),
                                           start=True, stop=True)
                        return ins
                    sbanks = [bankT[bss[0]]] + ([bankT[bss[1]]] if nb > 2 else [])
                    pg.op("pe", score, reads=[t_kT[buf], t_qT[buf]], writes=sbanks)
                    pg.op("act", lambda e: e.activation(
                        out=A16(o_pT[pi], [[1, nb * 2 * P]]), in_=PS32(sb0, [[1, nb * 2 * P]]), func=AF.Exp, scale=0.125),
                        reads=sbanks, writes=[t_pT[pi]])
                    mlist = [(i, mk) for i, (jb, mk) in enumerate(blocks) if mk is not None]
                    meng = cfg.get("mask_engs", ("dve",))[ctr["m"] % len(cfg.get("mask_engs", ("dve",)))]
                    ctr["m"] += 1
                    if len(mlist) == 2:
                        i0, mk0 = mlist[0]
                        i1, mk1 = mlist[1]
                        assert mk1 == mk0 + 1
                        pa = A16(o_pT[pi] + i0 * 2 * P * 2, [[(i1 - i0) * 2 * P, 2], [P, 2], [1, P]])
                        ma = A16(o_mask + mk0 * P * 2, [[P, 2], [0, 2], [1, P]])
                    else:
                        i0, mk0 = mlist[0]
                        pa = A16(o_pT[pi] + i0 * 2 * P * 2, [[P, 2], [1, P]])
                        ma = A16(o_mask + mk0 * P * 2, [[0, 2], [1, P]])
                    pg.op(meng, lambda e: e.tensor_tensor(out=pa, in0=pa, in1=ma, op=ALU.mult),
                          reads=[t_pT[pi], t_const], writes=[t_pT[pi]])

                def back():
                    pi = st["pi"]

                    def pv(e):
                        ins = None
                        for hi in range(2):
                            h = 2 * c + hi
                            vh = (h // 4) if kind == "A" else h
                            voff = (256 if kind == "A" else 512) + vh * 65
                            bo = bOs[h // 4]
                            for i, (jb, mk) in enumerate(blocks):
                                ins = e.matmul(out=PS32(bo, [[1, 65]], off=(h % 4) * 65),
                                               lhsT=A16(o_pT[pi] + (i * 2 + hi) * P * 2, [[1, P]]),
                                               rhs=A16(o_kv[buf] + ((jb - j0) * RAWK + voff) * 2, [[1, 65]]),
                                               start=(i == 0), stop=(i == nb - 1))
                        return ins
                    pg.op("pe", pv, reads=[t_pT[pi], t_kv[buf]], writes=[bankT[bOs[c // 2]]])
                    if False:
                        def fill(e):
                            ins = None
                            for _ in range(NFILL):
                                ins = e.matmul(out=PS32(3, [[1, 4 * P]]), lhsT=ident, rhs=A16(o_mask, [[1, 4 * P]]),
                                               start=True, stop=True)
                            return ins
                        pg.op("pe", fill, reads=[t_const], writes=[t_fill])
                    if c != 3:
                        return
                    nv = v1 - v0
                    if kind == "A":
                        ci = ctr["ca"] % NOB; ctr["ca"] += 1
                        for b2 in range(2):
                            pg.op("dve", lambda e, b2=b2: e.tensor_tensor(
                                out=A32(o_den + b2 * 16, [[1, 4]]), in0=PS32(bOs[b2], [[65, 4]], off=64),
                                in1=A32(o_es + b2 * 16, [[1, 4]]), op=ALU.add),
                                reads=[bankT[bOs[b2]], t_es], writes=[t_den])
                        pg.op("dve", lambda e: e.reciprocal(out=A32(o_den + 32, [[1, 8]]), in_=A32(o_den, [[1, 8]])),
                              reads=[t_den], writes=[t_den])
                        for b2 in range(2):
                            pg.op("dve", lambda e, b2=b2: e.tensor_tensor(
                                out=A16(o_ca[ci] + b2 * 256 * 2, [[64, 4], [1, 64]]),
                                in0=PS32(bOs[b2], [[65, 4], [1, 64]]),
                                in1=A32(o_den + 32 + b2 * 16, [[1, 4], [0, 64]]), op=ALU.mult),
                                reads=[bankT[bOs[b2]], t_den], writes=[t_ca[ci]])
                        dst = DR(oa_d, qs * 512, [[512, P], [1, 512]])
                        out_stores.append(pg.op(cfg.get("st_eng", "pool"), lambda e: e.dma_start(
                            out=dst, in_=A16(o_ca[ci], [[1, 512]])), reads=[t_ca[ci]], lane=l_ca[ci]))
                    else:
                        oi = ctr["ob"] % NOB; ctr["ob"] += 1
                        for b2 in range(2):
                            pg.op("dve", lambda e, b2=b2: e.tensor_copy(
                                A32(o_ob[oi] + b2 * 260 * 4, [[1, 260]]), PS32(bOs[b2], [[1, 260]])),
                                reads=[bankT[bOs[b2]]], writes=[t_ob[oi]])
                        dst = DR(ob_d, (pat * OWN + r + D * (qs + v0)) * 520, [[D * 520, nv], [1, 520]])
                        out_stores.append(pg.op(cfg.get("st_eng", "pool"), lambda e: e.dma_start(
                            out=dst, in_=A32(o_ob[oi], [[1, 520]], p0=v0, n=nv)), reads=[t_ob[oi]], lane=l_ob[oi]))
                return front, back

            def part_sizes(seg):
                kind, D, r, n0, n1, pat = seg
                Lk, Lq, j0, j1, q_lo = seg_geom(seg)
                return (j1 - j0 + 1), (n1 - n0)

            def seg_bases(parts):
                kb = qb = 0
                out = []
                for p_ in parts:
                    out.append((kb, qb))
                    nk, nq_ = part_sizes(p_)
                    kb += nk; qb += nq_
                assert kb <= NKB and qb <= NQB, (kb, qb)
                return out

            def seg_prepare(parts, buf):
                for p_, (kb, qb) in zip(parts, seg_bases(parts)):
                    yield from part_prepare(p_, buf, kb, qb)

            def seg_units(parts, buf):
                for p_, (kb, qb) in zip(parts, seg_bases(parts)):
                    yield from part_units(p_, buf, kb, qb)

            n_seg = cfg.get("n_seg", len(segs))
            segs = segs[:n_seg] if isinstance(n_seg, int) else [segs[i] for i in n_seg]
            LAG = cfg.get("lag", 4)
            prep = seg_prepare(segs[0], 0)
            for _ in prep:
                pass
            pend = []
            for si, seg in enumerate(segs):
                while pend:
                    pend.pop(0)()
                nxt = seg_prepare(segs[si + 1], (si + 1) % 2) if si + 1 < len(segs) else None
                for (front, back) in seg_units(seg, si % 2):
                    if nxt is not None:
                        next(nxt, None)
                    front()
                    pend.append(back)
                    if len(pend) > LAG:
                        pend.pop(0)()
                if nxt is not None:
                    for _ in nxt:
                        pass
            while pend:
                pend.pop(0)()
            sb.release()
            pg.set_fence()
            return out_stores

        st2 = []
        if stop_after not in ("ffn1", "p1b"):
            st2 = phase2(st1b)

        def phase2c(att_stores, n_sub, after_wo=None):
            sb.mark()
            o_wo = sb.alloc(NDC * DM * 2)
            t_wo = Tile()
            pg.op("pool", lambda e: e.dma_start(out=A16(o_wo, [[DM, NDC], [1, DM]]),
                                                in_=DR(wout_d, 0, [[DM, P], [P * DM, NDC], [1, DM]])),
                  writes=[t_wo], lane=pg.lane())
            if after_wo is not None:
                after_wo()
            o_cat = [sb.alloc(DM * 2) for _ in range(2)]
            o_ob3 = [sb.alloc(3 * 520 * 4) for _ in range(2)]
            o_x = [sb.alloc(DM * 4) for _ in range(2)]
            o_cT = [sb.alloc(NDC * P * 2) for _ in range(2)]
            o_rd = sb.alloc(8 * 4)
            t_cat = [Tile(), Tile()]; l_cat = [pg.lane(), pg.lane()]
            t_ob3 = [Tile(), Tile()]; l_ob3 = [pg.lane(), pg.lane()]
            t_x = [Tile(), Tile()]; l_x = [pg.lane(), pg.lane()]; l_xs = [pg.lane(), pg.lane()]
            t_cT = [Tile(), Tile()]
            t_rd = Tile()
            bT = 7
            stores = []

            def stage_a(k):
                sl = k % 2
                pg.op("sp", lambda e: e.dma_start(out=A16(o_cat[sl], [[1, 512]]),
                                                  in_=DR(oa_d, k * P * 512, [[512, P], [1, 512]])),
                      writes=[t_cat[sl]], lane=l_cat[sl], extra_deps=att_stores)
                pg.op("sp", lambda e: e.dma_start(out=A32(o_ob3[sl], [[520, 3], [1, 520]]),
                                                  in_=DR(ob_d, k * P * 520, [[520, P], [OWN * 520, 3], [1, 520]])),
                      writes=[t_ob3[sl]], lane=l_ob3[sl], extra_deps=att_stores)
                pg.op("sp", lambda e: e.dma_start(out=A32(o_x[sl], [[1, DM]]),
                                                  in_=DR(x1_d, k * P * DM, [[DM, P], [1, DM]])),
                      writes=[t_x[sl]], lane=l_x[sl], extra_deps=att_stores)
                o0 = A32(o_ob3[sl], [[1, 520]])
                pg.op("dve", lambda e: e.tensor_tensor(out=o0, in0=o0, in1=A32(o_ob3[sl] + 520 * 4, [[1, 520]]), op=ALU.add),
                      reads=[t_ob3[sl]], writes=[t_ob3[sl]])
                pg.op("dve", lambda e: e.tensor_tensor(out=o0, in0=o0, in1=A32(o_ob3[sl] + 2 * 520 * 4, [[1, 520]]), op=ALU.add),
                      reads=[t_ob3[sl]], writes=[t_ob3[sl]])
                pg.op("dve", lambda e: e.reciprocal(out=A32(o_rd, [[1, 8]]), in_=A32(o_ob3[sl] + 64 * 4, [[65, 8]])),
                      reads=[t_ob3[sl]], writes=[t_rd])
                pg.op("dve", lambda e: e.tensor_tensor(out=A16(o_cat[sl] + 512 * 2, [[64, 8], [1, 64]]),
                                                       in0=A32(o_ob3[sl], [[65, 8], [1, 64]]),
                                                       in1=A32(o_rd, [[1, 8], [0, 64]]), op=ALU.mult),
                      reads=[t_ob3[sl], t_rd], writes=[t_cat[sl]])

            def stage_t(k):
                sl = k % 2

                def tr(e):
                    ins = None
                    for dc in range(NDC):
                        ins = e.transpose(out=PS16(bT, [[1, P]], off=dc * P),
                                          in_=A16(o_cat[sl] + dc * P * 2, [[1, P]]), identity=ident)
                    return ins
                pg.op("pe", tr, reads=[t_cat[sl], t_const], writes=[bankT[bT]])
                pg.op("act", lambda e: e.copy(out=A16(o_cT[sl], [[1, NDC * P]]), in_=PS16(bT, [[1, NDC * P]])),
                      reads=[bankT[bT]], writes=[t_cT[sl]])

            def stage_mm(k):
                sl = k % 2
                for hf in range(2):
                    bank = (2 * k + hf) % 4

                    def mm(e, hf=hf, bank=bank):
                        ins = None
                        for cc in range(NDC):
                            ins = e.matmul(out=PS32(bank, [[1, 512]]),
                                           lhsT=A16(o_cT[sl] + cc * P * 2, [[1, P]]),
                                           rhs=A16(o_wo + (cc * DM + hf * 512) * 2, [[1, 512]]),
                                           start=(cc == 0), stop=(cc == NDC - 1))
                        return ins
                    pg.op("pe", mm, reads=[t_cT[sl], t_wo], writes=[bankT[bank]])

            def stage_r(k):
                sl = k % 2
                for hf in range(2):
                    bank = (2 * k + hf) % 4
                    xh = A32(o_x[sl] + hf * 2048, [[1, 512]])
                    pg.op("dve", lambda e, bank=bank, xh=xh: e.tensor_tensor(out=xh, in0=PS32(bank, [[1, 512]]), in1=xh, op=ALU.add),
                          reads=[bankT[bank], t_x[sl]], writes=[t_x[sl]])
                dst = DR(x1_d, k * P * DM, [[DM, P], [1, DM]])
                stores.append(pg.op(cfg.get("st2c_eng", "pool"), lambda e: e.dma_start(out=dst, in_=A32(o_x[sl], [[1, DM]])),
                                    reads=[t_x[sl]], lane=l_xs[sl]))

            stage_a(0)
            stage_t(0)
            for k in range(n_sub):
                if k + 1 < n_sub:
                    stage_a(k + 1)
                stage_mm(k)
                if k + 1 < n_sub:
                    stage_t(k + 1)
                stage_r(k)
            sb.release()
            pg.set_fence()
            return stores

        if stop_after == "all":
            sb.mark()
            pre2, load2 = alloc_load_ffn_weights((wg2_d, wu2_d, wd2_d), defer=True)
            st2c = phase2c(st2, OWN // P, after_wo=load2)
            ffn_pass("ffn2", OWN // TT, (wg2_d, wu2_d, wd2_d), 2, x1_d, out_d, final_gain_k=3, src_dep=st2c, pre=pre2)
            sb.release()

        finals = [lst[-1] for dom, lst in pg.dom_ops.items() if dom.startswith("L")]
        pg.op("sp", lambda e: e.nop(), extra_deps=finals)
        pg.emit(nc, stack)
    return nc


def _masks():
    a = np.arange(P)[:, None]
    b = np.arange(P)[None, :]
    m_prev = (b >= a)
    m = np.zeros((P, 4 * P), np.float32)
    m[:, 0:P] = (b <= a)
    m[:, P:2 * P] = (a <= b)
    m[:, 2 * P:3 * P] = (b <= a)
    m[:, 3 * P:4 * P] = (a <= b)
    return m


_CACHE = {}


def _get_nc(cfg_key, cfg):
    if cfg_key not in _CACHE:
        _CACHE[cfg_key] = build_program(cfg)
    return _CACHE[cfg_key]


def make_in_maps(inputs):
    x = np.asarray(inputs["x"], np.float32)
    pos = np.asarray(inputs["positions"], np.int32)
    gains = np.stack([np.asarray(inputs["norm_ffn1"], np.float32)[0], np.asarray(inputs["norm_mix"], np.float32)[0],
                      np.asarray(inputs["norm_ffn2"], np.float32)[0], np.asarray(inputs["norm_final"], np.float32)], 0)
    invf = (1.0 / (10000.0 ** (np.arange(0, 64, 2, dtype=np.float32) / 64.0))).astype(np.float32)
    invf = np.ascontiguousarray(np.broadcast_to(invf[None, :], (P, 32)))
    ident = np.eye(P, dtype=np.float32)
    masks = _masks()
    common = {
        "invf": invf, "ident": ident, "masks": masks, "gains": np.ascontiguousarray(gains),
        "a_sink": np.asarray(inputs["a_sink"], np.float32).reshape(1, 8),
        "w_gate1": np.ascontiguousarray(inputs["w_gate1"][0], dtype=np.float32),
        "w_up1": np.ascontiguousarray(inputs["w_up1"][0], dtype=np.float32),
        "w_down1": np.ascontiguousarray(inputs["w_down1"][0], dtype=np.float32),
        "w_gate2": np.ascontiguousarray(inputs["w_gate2"][0], dtype=np.float32),
        "w_up2": np.ascontiguousarray(inputs["w_up2"][0], dtype=np.float32),
        "w_down2": np.ascontiguousarray(inputs["w_down2"][0], dtype=np.float32),
        "w_in": np.ascontiguousarray(inputs["w_in"][0], dtype=np.float32),
        "w_out": np.ascontiguousarray(inputs["w_out"][0], dtype=np.float32),
    }
    maps = []
    for c in range(8):
        b, hf = c // 2, c % 2
        if hf == 0:
            xs = x[b, 0:LOC]
            ps = pos[b, 0:LOC]
        else:
            xs = x[b, SEQ - LOC:SEQ][::-1]
            ps = pos[b, SEQ - LOC:SEQ][::-1]
        m = dict(common)
        m["x"] = np.ascontiguousarray(xs)
        m["pos"] = np.ascontiguousarray(ps.reshape(LOC // P, P).T)
        maps.append(m)
    return maps


def kernel(**inputs):
    nc = _get_nc("full", {})
    maps = make_in_maps(inputs)
    res = run_bass_kernel_spmd(nc, maps, core_ids=list(range(8)))
    out = np.empty((BATCH, SEQ, DM), np.float32)
    for c in range(8):
        b, hf = c // 2, c % 2
        o = np.asarray(res.results[c]["out"], np.float32)
        if hf == 0:
            out[b, 0:OWN] = o
        else:
            out[b, SEQ - OWN:SEQ] = o[::-1]
    return out
```

```python
import contextlib
import numpy as np
import ml_dtypes
import concourse.bass as bass
import concourse.mybir as mybir
from concourse.bass_utils import run_bass_kernel_spmd

F32 = mybir.dt.float32
BF16 = mybir.dt.bfloat16
I32 = mybir.dt.int32
ALU = mybir.AluOpType
AF = mybir.ActivationFunctionType

P = 128
DM = 1024
DFF = 2816
NFC = 22
NDC = 8
SEQ = 8192
BATCH = 4
OWN = 4096
LOC = 5120
TT = 512
EPS = 1e-6
INW = 2304
C_AQ = 0
C_AK = 512
C_AV = 768
C_BQ = 900
C_BK = 1412
C_BV = 1924
ROWW = 2444

SB_BYTES = 206 * 1024
ROW32 = SB_BYTES // 4
ROW16 = SB_BYTES // 2


_FENCE = {}


class Tile:
    __slots__ = ("w", "r", "name")

    def __init__(self, name=""):
        self.w = {}
        self.r = dict(_FENCE)
        self.name = name


class Op:
    __slots__ = ("eng", "fn", "deps", "inc", "dom", "val", "is_dma", "idx")


COMPUTE = ("pe", "act", "dve", "pool")
ENGS = ("pe", "act", "dve", "pool", "sp")


class Prog:
    def __init__(self):
        self.ops = {e: [] for e in ENGS}
        self.dom_ops = {}
        self.nlanes = 0

    def lane(self):
        self.nlanes += 1
        return "L%d" % self.nlanes

    def op(self, eng, fn, reads=(), writes=(), lane=None, extra_deps=()):
        o = Op()
        o.eng = eng
        o.fn = fn
        o.is_dma = lane is not None
        o.dom = lane if lane is not None else eng
        o.inc = o.is_dma
        o.val = None
        deps = {}

        def add(p, kind):
            if p.dom == o.dom and not o.is_dma:
                if eng == "pe" or kind != "raw":
                    return
            q = deps.get(p.dom)
            if q is None or p.idx > q.idx:
                deps[p.dom] = p

        for t in reads:
            for p in t.w.values():
                add(p, "raw")
        for t in writes:
            for p in t.w.values():
                add(p, "waw")
            for p in t.r.values():
                add(p, "war")
        for p in extra_deps:
            add(p, "raw")
        o.deps = list(deps.values())
        for p in o.deps:
            p.inc = True
        lst = self.dom_ops.setdefault(o.dom, [])
        o.idx = len(lst)
        lst.append(o)
        self.ops[eng].append(o)
        for t in reads:
            t.r[o.dom] = o
        for t in writes:
            t.w[o.dom] = o
        return o

    def set_fence(self):
        _FENCE.clear()
        for dom, lst in self.dom_ops.items():
            _FENCE[dom] = lst[-1]

    def finalize(self):
        for dom, lst in self.dom_ops.items():
            c = 0
            for o in lst:
                if o.inc:
                    c += 16 if o.is_dma else 1
                    o.val = c

    def emit(self, nc, stack):
        self.finalize()
        sems = {}
        for dom in self.dom_ops:
            sems[dom] = stack.enter_context(nc.semaphore("s_" + dom))
        block = stack.enter_context(nc.Block())
        prog = self

        def run(eng_name):
            def body(e):
                seen = {}
                for o in prog.ops[eng_name]:
                    for d in o.deps:
                        if seen.get(d.dom, 0) >= d.val:
                            continue
                        e.wait_ge(sems[d.dom], d.val)
                        seen[d.dom] = d.val
                    ins = o.fn(e)
                    if o.inc:
                        ins.then_inc(sems[o.dom], 16 if o.is_dma else 1)
            return body

        block.tensor(run("pe"))
        block.scalar(run("act"))
        block.vector(run("dve"))
        block.gpsimd(run("pool"))
        block.sync(run("sp"))


class SbAlloc:
    def __init__(self, limit):
        self.off = 0
        self.limit = limit
        self.marks = []

    def alloc(self, nbytes):
        o = (self.off + 63) // 64 * 64
        self.off = o + nbytes
        assert self.off <= self.limit, "SBUF overflow %d" % self.off
        return o

    def mark(self):
        self.marks.append(self.off)

    def release(self):
        self.off = self.marks.pop()


def build_program(cfg):
    n_tiles1 = cfg.get("n_tiles1", LOC // TT)
    stop_after = cfg.get("stop_after", "all")
    nc = bass.Bass("TRN2", target_bir_lowering=False)
    dr = {}

    def din(name, shape, dt=F32):
        dr[name] = nc.dram_tensor(name, shape, dt, kind="ExternalInput")
        return dr[name]

    x_d = din("x", [LOC, DM])
    pos_d = din("pos", [P, LOC // P], I32)
    invf_d = din("invf", [P, 32])
    ident_d = din("ident", [P, P])
    mask_d = din("masks", [P, 4 * P])
    gains_d = din("gains", [4, DM])
    sink_d = din("a_sink", [1, 8])
    wg1_d = din("w_gate1", [DM, DFF]); wu1_d = din("w_up1", [DM, DFF]); wd1_d = din("w_down1", [DFF, DM])
    wg2_d = din("w_gate2", [DM, DFF]); wu2_d = din("w_up2", [DM, DFF]); wd2_d = din("w_down2", [DFF, DM])
    win_d = din("w_in", [DM, INW])
    wout_d = din("w_out", [DM, DM])
    out_d = nc.dram_tensor("out", [OWN, DM], F32, kind="ExternalOutput")
    dbg = cfg.get("debug", False)
    kind_s = "ExternalOutput" if dbg else "Internal"
    x1_d = nc.dram_tensor("x1s", [LOC, DM], F32, kind=kind_s)
    qkv_d = nc.dram_tensor("qkvs", [LOC, ROWW], BF16, kind=kind_s)
    oa_d = nc.dram_tensor("oas", [OWN, 512], BF16, kind=kind_s)
    ob_d = nc.dram_tensor("obs", [3, OWN, 520], F32, kind=kind_s)

    pg = Prog()
    _FENCE.clear()
    stack = contextlib.ExitStack()
    with stack:
        big32 = stack.enter_context(nc.sbuf_tensor("big", [P, ROW32], F32))
        big16 = big32.bitcast(BF16)
        bigi = big32.bitcast(I32)
        ps32 = stack.enter_context(nc.psum_tensor("ps", [P, 4096], F32))
        ps16 = ps32.bitcast(BF16)

        def A32(off, dims, p0=0, n=P):
            assert off % 4 == 0
            return bass.AP(big32, p0 * ROW32 + off // 4, [[ROW32, n]] + [list(d) for d in dims])

        def A16(off, dims, p0=0, n=P):
            assert off % 2 == 0
            return bass.AP(big16, p0 * ROW16 + off // 2, [[ROW16, n]] + [list(d) for d in dims])

        def AI(off, dims, p0=0, n=P):
            return bass.AP(bigi, p0 * ROW32 + off // 4, [[ROW32, n]] + [list(d) for d in dims])

        def PS32(bank, dims, off=0, p0=0, n=P):
            return bass.AP(ps32, p0 * 4096 + bank * 512 + off, [[4096, n]] + [list(d) for d in dims])

        def PS16(bank, dims, off=0, p0=0, n=P):
            return bass.AP(ps16, p0 * 8192 + bank * 1024 + off, [[8192, n]] + [list(d) for d in dims])

        def DR(t, off, dims):
            return bass.AP(t, off, [list(d) for d in dims])

        sb = SbAlloc(SB_BYTES)
        bankT = [Tile("bank%d" % i) for i in range(8)]

        o_ident = sb.alloc(P * 2)
        o_mask = sb.alloc(4 * P * 2)
        o_small = sb.alloc(1024)
        t_const = Tile("const")
        ident = A16(o_ident, [[1, P]])

        lc = pg.lane()
        pg.op("pool", lambda e: e.dma_start(out=A16(o_ident, [[1, P]]), in_=ident_d.ap()), writes=[t_const], lane=lc)
        lc2 = pg.lane()
        pg.op("pool", lambda e: e.dma_start(out=A16(o_mask, [[1, 4 * P]]), in_=mask_d.ap()), writes=[t_const], lane=lc2)
        def load_gain(k):
            o = sb.alloc(DM * 4)
            t = Tile("gain%d" % k)
            pg.op("sp", lambda e: e.dma_start(out=A32(o, [[1, DM]]), in_=DR(gains_d, k * DM, [[0, P], [1, DM]])),
                  writes=[t], lane=pg.lane())
            return A32(o, [[1, DM]]), t

        small_ctr = [0]

        def small_col():
            c = small_ctr[0] % 256
            small_ctr[0] += 1
            return o_small + 4 * c

        def load_weights_ffn(wg_d, wu_d, wd_d, o_wg, o_wu, o_wd, tiles, parts=("gu", "d"), d_deps=()):
            for g in (range(4) if "gu" in parts else ()):
                for (w_d, o_w, key) in ((wg_d, o_wg, "g"), (wu_d, o_wu, "u")):
                    src = DR(w_d, g * 704, [[DFF, P], [P * DFF, NDC], [1, 704]])
                    dst = A16(o_w + g * 704 * 2, [[DFF, NDC], [1, 704]])
                    pg.op("pool", (lambda e, s=src, d=dst: e.dma_start(out=d, in_=s)),
                          writes=[tiles[key][g]], lane=pg.lane())
            for hf in (range(2) if "d" in parts else ()):
                src = DR(wd_d, hf * 11 * P * DM, [[DM, P], [P * DM, 11], [1, DM]])
                dst = A16(o_wd + hf * 11 * DM * 2, [[DM, 11], [1, DM]])
                pg.op("pool", (lambda e, s=src, d=dst: e.dma_start(out=d, in_=s)),
                      writes=[tiles["d"][hf]], lane=pg.lane(), extra_deps=d_deps)

        def rms_stats(x_ap, t_x, t_stat, col, junk_ap, t_junk):
            ss = A32(col, [[1, 1]])
            pg.op("act", lambda e: e.activation(out=junk_ap, in_=x_ap, func=AF.Square,
                                                scale=1.0 / 32.0, accum_out=ss),
                  reads=[t_x], writes=[t_stat, t_junk])
            pg.op("act", lambda e: e.activation(out=ss, in_=ss, func=AF.Sqrt, bias=EPS, scale=1.0),
                  reads=[t_stat], writes=[t_stat])
            pg.op("dve", lambda e: e.reciprocal(out=ss, in_=ss), reads=[t_stat], writes=[t_stat])
            return ss

        def alloc_load_ffn_weights(wdr, defer=False):
            o_wg = sb.alloc(NDC * DFF * 2)
            o_wu = sb.alloc(NDC * DFF * 2)
            o_wd = sb.alloc(NFC * DM * 2)
            wt = {"g": [Tile() for _ in range(4)], "u": [Tile() for _ in range(4)], "d": [Tile() for _ in range(2)]}
            if defer:
                return (o_wg, o_wu, o_wd, wt), (lambda parts=("gu", "d"), d_deps=(): load_weights_ffn(
                    wdr[0], wdr[1], wdr[2], o_wg, o_wu, o_wd, wt, parts=parts, d_deps=d_deps))
            load_weights_ffn(wdr[0], wdr[1], wdr[2], o_wg, o_wu, o_wd, wt)
            return o_wg, o_wu, o_wd, wt

        def ffn_pass(name, n_tiles, wdr, gain_k, src_d, dst_d, final_gain_k=None, src_dep=None, pre=None, tail_hook=None):
            sb.mark()
            if pre is None:
                o_wg, o_wu, o_wd, wt = alloc_load_ffn_weights(wdr)
            else:
                o_wg, o_wu, o_wd, wt = pre
            o_xn = [sb.alloc(DM * 4) for _ in range(2)]
            o_xr = [sb.alloc(DM * 4) for _ in range(2)]
            o_h = [sb.alloc(DM * 2) for _ in range(2)]
            o_hT = [sb.alloc(NDC * TT * 2) for _ in range(2)]
            o_act = sb.alloc(NFC * TT * 2)
            o_sg = [sb.alloc(TT * 4)] * 2
            o_junk2 = sb.alloc(DM * 2)
            t_junk2 = Tile()
            gain_a, t_gain = load_gain(gain_k)
            if final_gain_k is not None:
                fgain_a, t_fgain = load_gain(final_gain_k)
            t_xn = [Tile() for _ in range(2)]; l_xn = [pg.lane() for _ in range(2)]
            t_xr = [Tile() for _ in range(2)]; l_xr = [pg.lane() for _ in range(2)]
            l_st = [pg.lane() for _ in range(2)]
            t_h = [Tile() for _ in range(2)]
            t_hT = [[Tile() for _ in range(4)] for _ in range(2)]
            t_act = [Tile() for _ in range(NFC)]
            t_sg = [Tile()] * 2
            t_stat = Tile()
            bG = [0, 1]; bU = [2, 3]; bD = [4, 5, 6]; bT = 7
            cnt = {"xn": 0, "xr": 0, "d": 0}

            def norm_sub(i, s, part):
                k = i * 4 + s
                sl = k % 2
                hb = i % 2
                if part == 0:
                    src = DR(src_d, (i * TT + s * P) * DM, [[DM, P], [1, DM]])
                    xa = A32(o_xn[sl], [[1, DM]])
                    pg.op("sp", lambda e: e.dma_start(out=xa, in_=src), writes=[t_xn[sl]], lane=l_xn[sl],
                          extra_deps=([src_dep[k]] if src_dep else ()))
                    col = small_col()
                    ha = A16(o_h[sl], [[1, DM]])
                    ss = rms_stats(xa, t_xn[sl], t_stat, col, ha, t_h[sl])
                    pg.op("dve", lambda e: e.scalar_tensor_tensor(out=ha, in0=xa, scalar=ss, in1=gain_a,
                                                                  op0=ALU.mult, op1=ALU.mult),
                          reads=[t_xn[sl], t_stat, t_gain], writes=[t_h[sl]])
                else:
                    def tr(e):
                        ins = None
                        for dc in range(NDC):
                            ins = e.transpose(out=PS16(bT, [[1, P]], off=dc * P),
                                              in_=A16(o_h[sl] + dc * P * 2, [[1, P]]), identity=ident)
                        return ins
                    pg.op("pe", tr, reads=[t_h[sl], t_const], writes=[bankT[bT]])
                    dst = A16(o_hT[hb] + s * P * 2, [[TT, NDC], [1, P]])
                    pg.op("act", lambda e: e.copy(out=dst, in_=PS16(bT, [[P, NDC], [1, P]])),
                          reads=[bankT[bT]], writes=[t_hT[hb][s]])

            def gate_up(i, f):
                hb = i % 2
                g = f * P // 704
                g2 = (f * P + P - 1) // 704
                wtiles = list({wt["g"][g], wt["g"][g2], wt["u"][g], wt["u"][g2]})

                def mm(e, o_w, bank):
                    ins = None
                    for dc in range(NDC):
                        ins = e.matmul(out=PS32(bank, [[1, TT]]),
                                       lhsT=A16(o_w + (dc * DFF + f * P) * 2, [[1, P]]),
                                       rhs=A16(o_hT[hb] + dc * TT * 2, [[1, TT]]),
                                       start=(dc == 0), stop=(dc == NDC - 1))
                    return ins
                bg = bG[f % 2]; bu = bU[f % 2]
                pg.op("pe", lambda e: mm(e, o_wg, bg), reads=t_hT[hb] + wtiles, writes=[bankT[bg]])
                pg.op("pe", lambda e: mm(e, o_wu, bu), reads=t_hT[hb] + wtiles, writes=[bankT[bu]])
                sg = A32(o_sg[f % 2], [[1, TT]])
                pg.op("act", lambda e: e.activation(out=sg, in_=PS32(bg, [[1, TT]]), func=AF.Silu),
                      reads=[bankT[bg]], writes=[t_sg[f % 2]])
                pg.op("dve", lambda e: e.tensor_tensor(out=A16(o_act + f * TT * 2, [[1, TT]]), in0=sg,
                                                       in1=PS32(bu, [[1, TT]]), op=ALU.mult),
                      reads=[t_sg[f % 2], bankT[bu]], writes=[t_act[f]])

            def down_sub(i, s):
                k = i * 4 + s
                sl = k % 2
                row0 = i * TT + s * P
                src = DR(src_d, row0 * DM, [[DM, P], [1, DM]])
                xr = A32(o_xr[sl], [[1, DM]])
                pg.op("sp", lambda e: e.dma_start(out=xr, in_=src), writes=[t_xr[sl]], lane=l_xr[sl],
                      extra_deps=([src_dep[k]] if src_dep else ()))
                for hf in range(2):
                    bank = bD[cnt["d"] % 3]
                    cnt["d"] += 1

                    def mm(e, bank=bank, hf=hf):
                        ins = None
                        for f in range(NFC):
                            ins = e.matmul(out=PS32(bank, [[1, 512]]),
                                           lhsT=A16(o_act + (f * TT + s * P) * 2, [[1, P]]),
                                           rhs=A16(o_wd + (f * DM + hf * 512) * 2, [[1, 512]]),
                                           start=(f == 0), stop=(f == NFC - 1))
                        return ins
                    pg.op("pe", mm, reads=t_act + wt["d"], writes=[bankT[bank]])
                    xh = A32(o_xr[sl] + hf * 2048, [[1, 512]])
                    pg.op("dve", lambda e, bank=bank, xh=xh: e.scalar_tensor_tensor(
                        out=xh, in0=PS32(bank, [[1, 512]]), scalar=0.5, in1=xh, op0=ALU.mult, op1=ALU.add),
                        reads=[bankT[bank], t_xr[sl]], writes=[t_xr[sl]])
                if final_gain_k is not None:
                    col = small_col()
                    ss = rms_stats(xr, t_xr[sl], t_stat, col, A16(o_junk2, [[1, DM]]), t_junk2)
                    pg.op("dve", lambda e: e.scalar_tensor_tensor(out=xr, in0=xr, scalar=ss, in1=fgain_a,
                                                                  op0=ALU.mult, op1=ALU.mult),
                          reads=[t_xr[sl], t_stat, t_fgain], writes=[t_xr[sl]])
                dst = DR(dst_d, row0 * DM, [[DM, P], [1, DM]])
                return pg.op("pool", lambda e: e.dma_start(out=dst, in_=xr), reads=[t_xr[sl]], lane=l_st[sl])

            stores = []
            for s in range(4):
                norm_sub(0, s, 0)
                norm_sub(0, s, 1)
            for i in range(n_tiles):
                for f in range(NFC):
                    gate_up(i, f)
                    if i + 1 < n_tiles:
                        if f in (1, 6, 11, 16):
                            norm_sub(i + 1, (f - 1) // 5, 0)
                        if f in (5, 10, 15, 20):
                            norm_sub(i + 1, (f - 5) // 5, 1)
                if tail_hook is not None and i == n_tiles - 1:
                    tail_hook(o_wg, pg.ops["pe"][-1])
                for s in range(4):
                    stores.append(down_sub(i, s))
            sb.release()
            pg.set_fence()
            return stores

        win_pre = {}

        def load_win(o_win, extra):
            t_win = [Tile(), Tile()]
            for hf in range(2):
                src = DR(win_d, hf * 1152, [[INW, P], [P * INW, NDC], [1, 1152]])
                dst = A16(o_win + hf * 1152 * 2, [[INW, NDC], [1, 1152]])
                pg.op("pool", (lambda e, s=src, d=dst: e.dma_start(out=d, in_=s)), writes=[t_win[hf]], lane=pg.lane(),
                      extra_deps=extra)
            return t_win

        def win_hook(o_wg, last_pe_op):
            win_pre["o"] = o_wg
            win_pre["t"] = load_win(o_wg, [last_pe_op])

        st1 = ffn_pass("ffn1", n_tiles1, (wg1_d, wu1_d, wd1_d), 0, x_d, x1_d,
                       tail_hook=(win_hook if cfg.get("win_prefetch", True) and stop_after != "ffn1" else None))
        n_sub1 = n_tiles1 * 4

        def phase1b(n_sub, x1_stores):
            import math
            sb.mark()
            o_win = sb.alloc(NDC * INW * 2)
            if "o" in win_pre:
                assert win_pre["o"] == o_win, (win_pre["o"], o_win)
                t_win = win_pre["t"]
            else:
                t_win = load_win(o_win, [])
            gain_a, t_gain = load_gain(1)
            NS = LOC // P
            o_pos = sb.alloc(NS * 4); o_posf = sb.alloc(NS * 4); o_invf = sb.alloc(32 * 4)
            o_ang = sb.alloc(NS * 32 * 4); o_u = sb.alloc(NS * 32 * 4)
            o_cos = sb.alloc(NS * 64 * 4); o_sin = sb.alloc(NS * 64 * 4)
            t_rope = Tile()
            pg.op("sp", lambda e: e.dma_start(out=AI(o_pos, [[1, NS]]), in_=pos_d.ap()), writes=[t_rope], lane=pg.lane())
            pg.op("sp", lambda e: e.dma_start(out=A32(o_invf, [[1, 32]]), in_=invf_d.ap()), writes=[t_rope], lane=pg.lane())
            pg.op("dve", lambda e: e.tensor_copy(A32(o_posf, [[1, NS]]), AI(o_pos, [[1, NS]])), reads=[t_rope], writes=[t_rope])
            pg.op("dve", lambda e: e.tensor_tensor(out=A32(o_ang, [[32, NS], [1, 32]]), in0=A32(o_posf, [[1, NS], [0, 32]]),
                                                   in1=A32(o_invf, [[0, NS], [1, 32]]), op=ALU.mult),
                  reads=[t_rope], writes=[t_rope])
            C1 = 6.28125
            C2 = 2 * math.pi - C1
            o_v = sb.alloc(NS * 32 * 4); o_ki = sb.alloc(NS * 32 * 4); o_kf = sb.alloc(NS * 32 * 4); o_m = sb.alloc(NS * 32 * 4)
            NE = NS * 32

            def reduce_angle(shift):
                fl = lambda o: A32(o, [[1, NE]])
                pg.op("dve", lambda e: e.tensor_scalar(out=fl(o_v), in0=fl(o_ang), scalar1=1.0 / (2 * math.pi),
                                                       scalar2=0.5 + shift / (2 * math.pi), op0=ALU.mult, op1=ALU.add),
                      reads=[t_rope], writes=[t_rope])
                pg.op("dve", lambda e: e.tensor_copy(AI(o_ki, [[1, NE]]), fl(o_v)), reads=[t_rope], writes=[t_rope])
                pg.op("dve", lambda e: e.tensor_copy(fl(o_kf), AI(o_ki, [[1, NE]])), reads=[t_rope], writes=[t_rope])
                pg.op("dve", lambda e: e.scalar_tensor_tensor(out=fl(o_u), in0=fl(o_kf), scalar=-C1, in1=fl(o_ang),
                                                              op0=ALU.mult, op1=ALU.add), reads=[t_rope], writes=[t_rope])
                pg.op("dve", lambda e: e.scalar_tensor_tensor(out=fl(o_u), in0=fl(o_kf), scalar=-C2, in1=fl(o_u),
                                                              op0=ALU.mult, op1=ALU.add), reads=[t_rope], writes=[t_rope])
                pg.op("dve", lambda e: e.tensor_scalar(out=fl(o_m), in0=fl(o_u), scalar1=-math.pi - shift, scalar2=None,
                                                       op0=ALU.is_lt), reads=[t_rope], writes=[t_rope])
                pg.op("dve", lambda e: e.scalar_tensor_tensor(out=fl(o_u), in0=fl(o_m), scalar=2 * math.pi, in1=fl(o_u),
                                                              op0=ALU.mult, op1=ALU.add), reads=[t_rope], writes=[t_rope])

            reduce_angle(0.0)
            pg.op("act", lambda e: e.activation(out=A32(o_sin + 32 * 4, [[64, NS], [1, 32]]), in_=A32(o_u, [[32, NS], [1, 32]]),
                                                func=AF.Sin, bias=0.0, scale=1.0), reads=[t_rope], writes=[t_rope])
            pg.op("act", lambda e: e.activation(out=A32(o_sin, [[64, NS], [1, 32]]), in_=A32(o_u, [[32, NS], [1, 32]]),
                                                func=AF.Sin, bias=0.0, scale=-1.0), reads=[t_rope], writes=[t_rope])
            reduce_angle(math.pi / 2)
            pg.op("act", lambda e: e.activation(out=A32(o_cos, [[64, NS], [1, 32]]), in_=A32(o_u, [[32, NS], [1, 32]]),
                                                func=AF.Sin, bias=math.pi / 2, scale=1.0), reads=[t_rope], writes=[t_rope])
            pg.op("act", lambda e: e.activation(out=A32(o_cos + 32 * 4, [[64, NS], [1, 32]]), in_=A32(o_u, [[32, NS], [1, 32]]),
                                                func=AF.Sin, bias=math.pi / 2, scale=1.0), reads=[t_rope], writes=[t_rope])

            o_xa = [sb.alloc(DM * 4) for _ in range(2)]
            o_hb = [sb.alloc(DM * 2) for _ in range(2)]
            o_hT2 = [sb.alloc(NDC * P * 2) for _ in range(2)]
            o_tA = [sb.alloc(512 * 4) for _ in range(2)]
            o_tB = [sb.alloc(512 * 4) for _ in range(2)]
            o_row = [sb.alloc(ROWW * 2) for _ in range(2)]
            t_xa = [Tile(), Tile()]; l_xa = [pg.lane(), pg.lane()]
            t_hb = [Tile(), Tile()]; t_hT2 = [Tile(), Tile()]
            t_tA = [Tile(), Tile()]; t_tB = [Tile(), Tile()]
            t_row = [Tile(), Tile()]; l_row = [pg.lane(), pg.lane()]
            t_stat = Tile()
            for sl in range(2):
                pg.op("pool", lambda e, sl=sl: e.memset(A16(o_row[sl] + (C_AV + 64) * 2, [[65, 2], [1, 1]]), 1.0), writes=[t_row[sl]])
                pg.op("pool", lambda e, sl=sl: e.memset(A16(o_row[sl] + (C_BV + 64) * 2, [[65, 8], [1, 1]]), 1.0), writes=[t_row[sl]])
            bT = 7
            grp = [(0, 512), (512, 512), (1024, 512), (1536, 512), (2048, 256)]
            cntj = [0]
            stores = []
            def stage_a(k):
                sl = k % 2
                src = DR(x1_d, k * P * DM, [[DM, P], [1, DM]])
                xa = A32(o_xa[sl], [[1, DM]])
                pg.op("sp", lambda e: e.dma_start(out=xa, in_=src), writes=[t_xa[sl]], lane=l_xa[sl],
                      extra_deps=[x1_stores[k]])
                ha = A16(o_hb[sl], [[1, DM]])
                ss = rms_stats(xa, t_xa[sl], t_stat, small_col(), ha, t_hb[sl])
                pg.op("dve", lambda e: e.scalar_tensor_tensor(out=ha, in0=xa, scalar=ss, in1=gain_a,
                                                              op0=ALU.mult, op1=ALU.mult),
                      reads=[t_xa[sl], t_stat, t_gain], writes=[t_hb[sl]])

            def stage_t(k):
                sl = k % 2

                def tr(e):
                    ins = None
                    for dc in range(NDC):
                        ins = e.transpose(out=PS16(bT, [[1, P]], off=dc * P),
                                          in_=A16(o_hb[sl] + dc * P * 2, [[1, P]]), identity=ident)
                    return ins
                pg.op("pe", tr, reads=[t_hb[sl], t_const], writes=[bankT[bT]])
                pg.op("act", lambda e: e.copy(out=A16(o_hT2[sl], [[1, NDC * P]]), in_=PS16(bT, [[1, NDC * P]])),
                      reads=[bankT[bT]], writes=[t_hT2[sl]])

            def stage_mm(k, gis):
                sl = k % 2
                for gi in gis:
                    c0, cw = grp[gi]

                    def mm(e, gi=gi, c0=c0, cw=cw):
                        ins = None
                        for dc in range(NDC):
                            ins = e.matmul(out=PS32(gi, [[1, cw]]),
                                           lhsT=A16(o_hT2[sl] + dc * P * 2, [[1, P]]),
                                           rhs=A16(o_win + (dc * INW + c0) * 2, [[1, cw]]),
                                           start=(dc == 0), stop=(dc == NDC - 1))
                        return ins
                    pg.op("pe", mm, reads=[t_hT2[sl]] + t_win, writes=[bankT[gi]])

            def stage_c(k):
                sl = k % 2
                cosk = o_cos + k * 64 * 4
                sink = o_sin + k * 64 * 4

                def rope(bank, off, H, dst_c, dup=False):
                    j = cntj[0] % 2
                    cntj[0] += 1
                    tA = A32(o_tA[j], [[64, H], [1, 64]])
                    pg.op("dve", lambda e: e.tensor_tensor(out=tA, in0=PS32(bank, [[64, H], [1, 64]], off=off),
                                                           in1=A32(cosk, [[0, H], [1, 64]]), op=ALU.mult),
                          reads=[bankT[bank], t_rope], writes=[t_tA[j]])
                    pg.op("dve", lambda e: e.tensor_tensor(out=A32(o_tB[j], [[64, H], [1, 32]]),
                                                           in0=PS32(bank, [[64, H], [1, 32]], off=off + 32),
                                                           in1=A32(sink, [[0, H], [1, 32]]), op=ALU.mult),
                          reads=[bankT[bank], t_rope], writes=[t_tB[j]])
                    pg.op("dve", lambda e: e.tensor_tensor(out=A32(o_tB[j] + 32 * 4, [[64, H], [1, 32]]),
                                                           in0=PS32(bank, [[64, H], [1, 32]], off=off),
                                                           in1=A32(sink + 32 * 4, [[0, H], [1, 32]]), op=ALU.mult),
                          reads=[bankT[bank], t_rope], writes=[t_tB[j]])
                    if not dup:
                        pg.op("pool", lambda e: e.tensor_tensor(out=A16(o_row[sl] + dst_c * 2, [[64, H], [1, 64]]), in0=tA,
                                                                in1=A32(o_tB[j], [[64, H], [1, 64]]), op=ALU.add),
                              reads=[t_tA[j], t_tB[j]], writes=[t_row[sl]])
                    else:
                        for du in range(2):
                            pg.op("pool", lambda e, du=du: e.tensor_tensor(
                                out=A16(o_row[sl] + (C_AK + du * 64) * 2, [[128, H], [1, 64]]), in0=tA,
                                in1=A32(o_tB[j], [[64, H], [1, 64]]), op=ALU.add),
                                reads=[t_tA[j], t_tB[j]], writes=[t_row[sl]])

                rope(0, 0, 8, C_AQ)
                rope(1, 0, 2, None, dup=True)
                rope(1, 256, 4, C_BQ)
                pg.op("act", lambda e: e.copy(out=A16(o_row[sl] + C_AV * 2, [[65, 2], [1, 64]]),
                                              in_=PS32(1, [[64, 2], [1, 64]], off=128)),
                      reads=[bankT[1]], writes=[t_row[sl]])
                rope(2, 0, 8, C_BQ + 256)
                rope(3, 0, 4, C_BK + 256)
                pg.op("act", lambda e: e.copy(out=A16(o_row[sl] + C_BV * 2, [[65, 4], [1, 64]]),
                                              in_=PS32(3, [[64, 4], [1, 64]], off=256)),
                      reads=[bankT[3]], writes=[t_row[sl]])
                pg.op("act", lambda e: e.copy(out=A16(o_row[sl] + (C_BV + 4 * 65) * 2, [[65, 4], [1, 64]]),
                                              in_=PS32(4, [[64, 4], [1, 64]], off=0)),
                      reads=[bankT[4]], writes=[t_row[sl]])
                dst = DR(qkv_d, k * P * ROWW, [[ROWW, P], [1, ROWW]])
                stores.append(pg.op("pool", lambda e: e.dma_start(out=dst, in_=A16(o_row[sl], [[1, ROWW]])),
                                    reads=[t_row[sl]], lane=l_row[sl]))

            stage_a(0)
            stage_t(0)
            for k in range(n_sub):
                if k + 1 < n_sub:
                    stage_a(k + 1)
                stage_mm(k, [0, 1, 2])
                if k + 1 < n_sub:
                    stage_t(k + 1)
                stage_mm(k, [3, 4])
                stage_c(k)
            sb.release()
            pg.set_fence()
            return stores

        st1b = []
        if stop_after != "ffn1":
            st1b = phase1b(n_sub1, st1)

        def phase2(qkv_stores):
            sb.mark()
            NKB = 12
            NQB = 12
            RAWK = 1032
            QTW = 4 * NQB * P
            o_es = sb.alloc(8 * 4)
            t_es = Tile()
            pg.op("sp", lambda e: e.dma_start(out=A32(o_es, [[1, 8]]), in_=DR(sink_d, 0, [[0, P], [1, 8]])),
                  writes=[t_es], lane=pg.lane())
            pg.op("act", lambda e: e.activation(out=A32(o_es, [[1, 8]]), in_=A32(o_es, [[1, 8]]), func=AF.Exp),
                  reads=[t_es], writes=[t_es])
            o_kv = [sb.alloc(NKB * RAWK * 2) for _ in range(2)]
            o_q = [sb.alloc(NQB * 512 * 2) for _ in range(2)]
            o_kT = [sb.alloc(4 * NKB * P * 2) for _ in range(2)]
            o_qT = [sb.alloc(2 * QTW * 2) for _ in range(2)]
            NPT = cfg.get("npt", 6)
            o_pT = [sb.alloc(6 * P * 2) for _ in range(NPT)]
            NOB = 4
            o_ob = [sb.alloc(520 * 4) for _ in range(NOB)]
            o_ca = [sb.alloc(512 * 2) for _ in range(NOB)]
            o_den = sb.alloc(16 * 4)
            t_kv = [Tile(), Tile()]; t_q = [Tile(), Tile()]
            l_kv = [[pg.lane(), pg.lane()] for _ in range(2)]; l_q = [[pg.lane(), pg.lane()] for _ in range(2)]
            t_kT = [Tile(), Tile()]; t_qT = [Tile(), Tile()]
            t_pT = [Tile() for _ in range(NPT)]
            t_ob = [Tile() for _ in range(NOB)]; l_ob = [pg.lane() for _ in range(NOB)]
            t_ca = [Tile() for _ in range(NOB)]; l_ca = [pg.lane() for _ in range(NOB)]
            t_den = Tile()
            t_fill = Tile()
            NFILL = cfg.get("fill", 0)
            bS = [(0, 1), (2, 3)]; bO = [(4, 5), (6, 4), (5, 6)]; bT = 7
            ctr = {"S": 0, "pT": 0, "ob": 0, "ca": 0, "m": 0, "qb": 0, "T": 0}
            bTs = (7, 3)
            out_stores = []
            for b in range(2):
                pg.op("pool", lambda e, b=b: e.memset(A16(o_kv[b], [[1, NKB * RAWK]]), 0.0), writes=[t_kv[b]])
                pg.op("pool", lambda e, b=b: e.memset(A16(o_q[b], [[1, NQB * 512]]), 0.0), writes=[t_q[b]])
                pg.op("pool", lambda e, b=b: e.memset(A16(o_kT[b], [[1, 4 * NKB * P]]), 0.0), writes=[t_kT[b]])
                pg.op("pool", lambda e, b=b: e.memset(A16(o_qT[b], [[1, 2 * QTW]]), 0.0), writes=[t_qT[b]])

            segs = []
            for s_ in range(4):
                segs.append([("A", 1, 0, 8 * s_, 8 * s_ + 8, None)])
            for (n0, n1) in ((0, 9), (9, 18), (18, 27), (27, 33)):
                segs.append([("B", 1, 0, n0, n1, 0)])
            for r in range(4):
                segs.append([("B", 4, r, 0, 9, 1)])
            for r in range(0, 16, 4):
                segs.append([("B", 16, r + i_, 0, 3, 2) for i_ in range(4)])

            def seg_geom(seg):
                kind, D, r, n0, n1, pat = seg
                Lk = LOC // D
                Lq = OWN // D
                if kind == "A":
                    j0 = max(0, n0 - 1); j1 = n1
                    q_lo = P * n0
                else:
                    j0 = max(0, n0 - 1); j1 = n1 - 1
                    q_lo = P * n0 - 64
                j1 = min(j1, (Lk - 1) // P)
                return Lk, Lq, j0, j1, q_lo

            def part_prepare(seg, buf, kb_base, qb_base):
                kind, D, r, n0, n1, pat = seg
                Lk, Lq, j0, j1, q_lo = seg_geom(seg)
                if kind == "A":
                    kc0, kcols, nkc, qc0 = C_AK, 256 + 130, 2, C_AQ
                else:
                    kc0, kcols, nkc, qc0 = C_BK, 512 + 520, 4, C_BQ
                k_lo = P * j0
                k_hi = min(Lk, P * (j1 + 1))
                nfull = (k_hi - k_lo) // P
                krem = (k_hi - k_lo) % P
                tok0 = r + D * k_lo
                if nfull:
                    src = DR(qkv_d, tok0 * ROWW + kc0, [[D * ROWW, P], [P * D * ROWW, nfull], [1, kcols]])
                    dst = A16(o_kv[buf] + kb_base * RAWK * 2, [[RAWK, nfull], [1, kcols]])
                    pg.op("sp", lambda e, dst=dst, src=src: e.dma_start(out=dst, in_=src), writes=[t_kv[buf]],
                          lane=l_kv[buf][0], extra_deps=qkv_stores)
                if krem:
                    src2 = DR(qkv_d, (tok0 + D * P * nfull) * ROWW + kc0, [[D * ROWW, krem], [1, kcols]])
                    dst2 = A16(o_kv[buf] + (kb_base + nfull) * RAWK * 2, [[1, kcols]], n=krem)
                    pg.op("sp", lambda e, dst2=dst2, src2=src2: e.dma_start(out=dst2, in_=src2), writes=[t_kv[buf]],
                          lane=l_kv[buf][1], extra_deps=qkv_stores)
                nkb = nfull + (1 if krem else 0)
                nqb = n1 - n0
                jq0 = 0
                if q_lo < 0:
                    src2 = DR(qkv_d, r * ROWW + qc0, [[D * ROWW, 64], [1, 512]])
                    dst2 = A16(o_q[buf] + qb_base * 512 * 2, [[1, 512]], p0=64, n=64)
                    pg.op("sp", lambda e, dst2=dst2, src2=src2: e.dma_start(out=dst2, in_=src2), writes=[t_q[buf]],
                          lane=l_q[buf][1], extra_deps=qkv_stores)
                    jq0 = 1
                if nqb - jq0 > 0:
                    qtok0 = r + D * (q_lo + P * jq0)
                    src = DR(qkv_d, qtok0 * ROWW + qc0, [[D * ROWW, P], [P * D * ROWW, nqb - jq0], [1, 512]])
                    dst = A16(o_q[buf] + (qb_base + jq0) * 512 * 2, [[512, nqb - jq0], [1, 512]])
                    pg.op("sp", lambda e, dst=dst, src=src: e.dma_start(out=dst, in_=src), writes=[t_q[buf]],
                          lane=l_q[buf][0], extra_deps=qkv_stores)
                yield
                for jj in range(nkb):
                    bT = bTs[ctr["T"] % 2]; ctr["T"] += 1

                    def trk(e, jj=jj, bT=bT):
                        ins = None
                        for c in range(nkc):
                            ins = e.transpose(out=PS16(bT, [[1, P]], off=c * P),
                                              in_=A16(o_kv[buf] + ((kb_base + jj) * RAWK + c * P) * 2, [[1, P]]), identity=ident)
                        return ins
                    pg.op("pe", trk, reads=[t_kv[buf], t_const], writes=[bankT[bT]])
                    if cfg.get("kevac_act", True):
                        pg.op("act", lambda e, jj=jj, bT=bT: e.copy(
                            out=A16(o_kT[buf] + (kb_base + jj) * P * 2, [[NKB * P, nkc], [1, P]]), in_=PS16(bT, [[P, nkc], [1, P]])),
                            reads=[bankT[bT]], writes=[t_kT[buf]])
                    else:
                        pg.op("dve", lambda e, jj=jj, bT=bT: e.tensor_copy(
                            A16(o_kT[buf] + (kb_base + jj) * P * 2, [[NKB * P, nkc], [1, P]]), PS16(bT, [[P, nkc], [1, P]])),
                            reads=[bankT[bT]], writes=[t_kT[buf]])
                    yield
                for jj in range(nqb):
                    bT = bTs[ctr["T"] % 2]; ctr["T"] += 1

                    def trq(e, jj=jj, bT=bT):
                        ins = None
                        for c in range(4):
                            ins = e.transpose(out=PS16(bT, [[1, P]], off=c * P),
                                              in_=A16(o_q[buf] + ((qb_base + jj) * 512 + c * P) * 2, [[1, P]]), identity=ident)
                        return ins
                    pg.op("pe", trq, reads=[t_q[buf], t_const], writes=[bankT[bT]])
                    for e_ in range(2):
                        pg.op("dve", lambda e, jj=jj, e_=e_, bT=bT: e.tensor_copy(
                            A16(o_qT[buf] + (e_ * QTW + (qb_base + jj) * P) * 2, [[NQB * P, 4], [1, P]], p0=64 * e_, n=64),
                            PS16(bT, [[P, 4], [1, P]], p0=64 * e_, n=64)),
                            reads=[bankT[bT]], writes=[t_qT[buf]])
                    yield

            def part_units(seg, buf, kb_base, qb_base):
                kind, D, r, n0, n1, pat = seg
                Lk, Lq, j0, j1, q_lo = seg_geom(seg)
                for n in range(n0, n1):
                    if kind == "A":
                        qs = P * n
                        blocks = [(jb, mk) for jb, mk in ((n - 1, 0), (n, None), (n + 1, 1)) if jb >= 0]
                        v0, v1 = 0, P
                    else:
                        qs = P * n - 64
                        blocks = [(jb, mk) for jb, mk in ((n - 1, 2), (n, 3)) if jb >= 0 and P * jb < Lk]
                        v0 = 64 if qs < 0 else 0
                        v1 = 64 if qs + P > Lq else P
                    qcol = qs - q_lo + qb_base * P
                    bOs = bO[ctr["qb"] % 3]
                    ctr["qb"] += 1
                    for c in range(4):
                        yield unit(kind, D, r, pat, buf, j0 - kb_base, blocks, qcol, qs, c, bOs, v0, v1)

            def unit(kind, D, r, pat, buf, j0, blocks, qcol, qs, c, bOs, v0, v1):
                nb = len(blocks)
                kc = (c // 2) if kind == "A" else c
                st = {}

                def front():
                    if nb > 2:
                        bss = ((0, 1), (1, 2))[ctr["S"] % 2]
                    else:
                        b1 = ctr["S"] % 3
                        bss = (b1, b1)
                    ctr["S"] += 1
                    pi = ctr["pT"] % NPT; ctr["pT"] += 1
                    st["pi"] = pi
                    sb0 = bss[0]

                    def score(e):
                        ins = None
                        for i, (jb, mk) in enumerate(blocks):
                            ins = e.matmul(out=PS32(sb0, [[1, 2 * P]], off=i * 2 * P),
                                           lhsT=A16(o_kT[buf] + (kc * NKB * P + (jb - j0) * P) * 2, [[1, P]]),
                                           rhs=A16(o_qT[buf] + (c * NQB * P + qcol) * 2, [[QTW, 2], [1, P]]),
                                           start=True, stop=True)
                        return ins
                    sbanks = [bankT[bss[0]]] + ([bankT[bss[1]]] if nb > 2 else [])
                    pg.op("pe", score, reads=[t_kT[buf], t_qT[buf]], writes=sbanks)
                    pg.op("act", lambda e: e.activation(
                        out=A16(o_pT[pi], [[1, nb * 2 * P]]), in_=PS32(sb0, [[1, nb * 2 * P]]), func=AF.Exp, scale=0.125),
                        reads=sbanks, writes=[t_pT[pi]])
                    mlist = [(i, mk) for i, (jb, mk) in enumerate(blocks) if mk is not None]
                    meng = cfg.get("mask_engs", ("dve",))[ctr["m"] % len(cfg.get("mask_engs", ("dve",)))]
                    ctr["m"] += 1
                    if len(mlist) == 2:
                        i0, mk0 = mlist[0]
                        i1, mk1 = mlist[1]
                        assert mk1 == mk0 + 1
                        pa = A16(o_pT[pi] + i0 * 2 * P * 2, [[(i1 - i0) * 2 * P, 2], [P, 2], [1, P]])
                        ma = A16(o_mask + mk0 * P * 2, [[P, 2], [0, 2], [1, P]])
                    else:
                        i0, mk0 = mlist[0]
                        pa = A16(o_pT[pi] + i0 * 2 * P * 2, [[P, 2], [1, P]])
                        ma = A16(o_mask + mk0 * P * 2, [[0, 2], [1, P]])
                    pg.op(meng, lambda e: e.tensor_tensor(out=pa, in0=pa, in1=ma, op=ALU.mult),
                          reads=[t_pT[pi], t_const], writes=[t_pT[pi]])

                def back():
                    pi = st["pi"]

                    def pv(e):
                        ins = None
                        for hi in range(2):
                            h = 2 * c + hi
                            vh = (h // 4) if kind == "A" else h
                            voff = (256 if kind == "A" else 512) + vh * 65
                            bo = bOs[h // 4]
                            for i, (jb, mk) in enumerate(blocks):
                                ins = e.matmul(out=PS32(bo, [[1, 65]], off=(h % 4) * 65),
                                               lhsT=A16(o_pT[pi] + (i * 2 + hi) * P * 2, [[1, P]]),
                                               rhs=A16(o_kv[buf] + ((jb - j0) * RAWK + voff) * 2, [[1, 65]]),
                                               start=(i == 0), stop=(i == nb - 1))
                        return ins
                    pg.op("pe", pv, reads=[t_pT[pi], t_kv[buf]], writes=[bankT[bOs[c // 2]]])
                    if False:
                        def fill(e):
                            ins = None
                            for _ in range(NFILL):
                                ins = e.matmul(out=PS32(3, [[1, 4 * P]]), lhsT=ident, rhs=A16(o_mask, [[1, 4 * P]]),
                                               start=True, stop=True)
                            return ins
                        pg.op("pe", fill, reads=[t_const], writes=[t_fill])
                    if c != 3:
                        return
                    nv = v1 - v0
                    if kind == "A":
                        ci = ctr["ca"] % NOB; ctr["ca"] += 1
                        for b2 in range(2):
                            pg.op("dve", lambda e, b2=b2: e.tensor_tensor(
                                out=A32(o_den + b2 * 16, [[1, 4]]), in0=PS32(bOs[b2], [[65, 4]], off=64),
                                in1=A32(o_es + b2 * 16, [[1, 4]]), op=ALU.add),
                                reads=[bankT[bOs[b2]], t_es], writes=[t_den])
                        pg.op("dve", lambda e: e.reciprocal(out=A32(o_den + 32, [[1, 8]]), in_=A32(o_den, [[1, 8]])),
                              reads=[t_den], writes=[t_den])
                        for b2 in range(2):
                            pg.op("dve", lambda e, b2=b2: e.tensor_tensor(
                                out=A16(o_ca[ci] + b2 * 256 * 2, [[64, 4], [1, 64]]),
                                in0=PS32(bOs[b2], [[65, 4], [1, 64]]),
                                in1=A32(o_den + 32 + b2 * 16, [[1, 4], [0, 64]]), op=ALU.mult),
                                reads=[bankT[bOs[b2]], t_den], writes=[t_ca[ci]])
                        dst = DR(oa_d, qs * 512, [[512, P], [1, 512]])
                        out_stores.append(pg.op(cfg.get("st_eng", "pool"), lambda e: e.dma_start(
                            out=dst, in_=A16(o_ca[ci], [[1, 512]])), reads=[t_ca[ci]], lane=l_ca[ci]))
                    else:
                        oi = ctr["ob"] % NOB; ctr["ob"] += 1
                        for b2 in range(2):
                            pg.op("dve", lambda e, b2=b2: e.tensor_copy(
                                A32(o_ob[oi] + b2 * 260 * 4, [[1, 260]]), PS32(bOs[b2], [[1, 260]])),
                                reads=[bankT[bOs[b2]]], writes=[t_ob[oi]])
                        dst = DR(ob_d, (pat * OWN + r + D * (qs + v0)) * 520, [[D * 520, nv], [1, 520]])
                        out_stores.append(pg.op(cfg.get("st_eng", "pool"), lambda e: e.dma_start(
                            out=dst, in_=A32(o_ob[oi], [[1, 520]], p0=v0, n=nv)), reads=[t_ob[oi]], lane=l_ob[oi]))
                return front, back

            def part_sizes(seg):
                kind, D, r, n0, n1, pat = seg
                Lk, Lq, j0, j1, q_lo = seg_geom(seg)
                return (j1 - j0 + 1), (n1 - n0)

            def seg_bases(parts):
                kb = qb = 0
                out = []
                for p_ in parts:
                    out.append((kb, qb))
                    nk, nq_ = part_sizes(p_)
                    kb += nk; qb += nq_
                assert kb <= NKB and qb <= NQB, (kb, qb)
                return out

            def seg_prepare(parts, buf):
                for p_, (kb, qb) in zip(parts, seg_bases(parts)):
                    yield from part_prepare(p_, buf, kb, qb)

            def seg_units(parts, buf):
                for p_, (kb, qb) in zip(parts, seg_bases(parts)):
                    yield from part_units(p_, buf, kb, qb)

            n_seg = cfg.get("n_seg", len(segs))
            segs = segs[:n_seg] if isinstance(n_seg, int) else [segs[i] for i in n_seg]
            LAG = cfg.get("lag", 3)
            prep = seg_prepare(segs[0], 0)
            for _ in prep:
                pass
            pend = []
            for si, seg in enumerate(segs):
                while pend:
                    pend.pop(0)()
                nxt = seg_prepare(segs[si + 1], (si + 1) % 2) if si + 1 < len(segs) else None
                for (front, back) in seg_units(seg, si % 2):
                    if nxt is not None:
                        next(nxt, None)
                    front()
                    pend.append(back)
                    if len(pend) > LAG:
                        pend.pop(0)()
                if nxt is not None:
                    for _ in nxt:
                        pass
            while pend:
                pend.pop(0)()
            sb.release()
            pg.set_fence()
            return out_stores

        st2 = []
        if stop_after not in ("ffn1", "p1b"):
            st2 = phase2(st1b)

        def phase2c(att_stores, n_sub, after_wo=None):
            sb.mark()
            o_wo = sb.alloc(NDC * DM * 2)
            t_wo = Tile()
            pg.op("pool", lambda e: e.dma_start(out=A16(o_wo, [[DM, NDC], [1, DM]]),
                                                in_=DR(wout_d, 0, [[DM, P], [P * DM, NDC], [1, DM]])),
                  writes=[t_wo], lane=pg.lane())
            if after_wo is not None:
                after_wo()
            o_cat = [sb.alloc(DM * 2) for _ in range(2)]
            o_ob3 = [sb.alloc(3 * 520 * 4) for _ in range(2)]
            o_x = [sb.alloc(DM * 4) for _ in range(2)]
            o_cT = [sb.alloc(NDC * P * 2) for _ in range(2)]
            o_rd = sb.alloc(8 * 4)
            t_cat = [Tile(), Tile()]; l_cat = [pg.lane(), pg.lane()]
            t_ob3 = [Tile(), Tile()]; l_ob3 = [pg.lane(), pg.lane()]
            t_x = [Tile(), Tile()]; l_x = [pg.lane(), pg.lane()]; l_xs = [pg.lane(), pg.lane()]
            t_cT = [Tile(), Tile()]
            t_rd = Tile()
            bT = 7
            stores = []

            def stage_a(k):
                sl = k % 2
                pg.op("sp", lambda e: e.dma_start(out=A16(o_cat[sl], [[1, 512]]),
                                                  in_=DR(oa_d, k * P * 512, [[512, P], [1, 512]])),
                      writes=[t_cat[sl]], lane=l_cat[sl], extra_deps=att_stores)
                pg.op("sp", lambda e: e.dma_start(out=A32(o_ob3[sl], [[520, 3], [1, 520]]),
                                                  in_=DR(ob_d, k * P * 520, [[520, P], [OWN * 520, 3], [1, 520]])),
                      writes=[t_ob3[sl]], lane=l_ob3[sl], extra_deps=att_stores)
                pg.op("sp", lambda e: e.dma_start(out=A32(o_x[sl], [[1, DM]]),
                                                  in_=DR(x1_d, k * P * DM, [[DM, P], [1, DM]])),
                      writes=[t_x[sl]], lane=l_x[sl], extra_deps=att_stores)
                o0 = A32(o_ob3[sl], [[1, 520]])
                pg.op("dve", lambda e: e.tensor_tensor(out=o0, in0=o0, in1=A32(o_ob3[sl] + 520 * 4, [[1, 520]]), op=ALU.add),
                      reads=[t_ob3[sl]], writes=[t_ob3[sl]])
                pg.op("dve", lambda e: e.tensor_tensor(out=o0, in0=o0, in1=A32(o_ob3[sl] + 2 * 520 * 4, [[1, 520]]), op=ALU.add),
                      reads=[t_ob3[sl]], writes=[t_ob3[sl]])
                pg.op("dve", lambda e: e.reciprocal(out=A32(o_rd, [[1, 8]]), in_=A32(o_ob3[sl] + 64 * 4, [[65, 8]])),
                      reads=[t_ob3[sl]], writes=[t_rd])
                pg.op("dve", lambda e: e.tensor_tensor(out=A16(o_cat[sl] + 512 * 2, [[64, 8], [1, 64]]),
                                                       in0=A32(o_ob3[sl], [[65, 8], [1, 64]]),
                                                       in1=A32(o_rd, [[1, 8], [0, 64]]), op=ALU.mult),
                      reads=[t_ob3[sl], t_rd], writes=[t_cat[sl]])

            def stage_t(k):
                sl = k % 2

                def tr(e):
                    ins = None
                    for dc in range(NDC):
                        ins = e.transpose(out=PS16(bT, [[1, P]], off=dc * P),
                                          in_=A16(o_cat[sl] + dc * P * 2, [[1, P]]), identity=ident)
                    return ins
                pg.op("pe", tr, reads=[t_cat[sl], t_const], writes=[bankT[bT]])
                pg.op("act", lambda e: e.copy(out=A16(o_cT[sl], [[1, NDC * P]]), in_=PS16(bT, [[1, NDC * P]])),
                      reads=[bankT[bT]], writes=[t_cT[sl]])

            def stage_mm(k):
                sl = k % 2
                for hf in range(2):
                    bank = (2 * k + hf) % 4

                    def mm(e, hf=hf, bank=bank):
                        ins = None
                        for cc in range(NDC):
                            ins = e.matmul(out=PS32(bank, [[1, 512]]),
                                           lhsT=A16(o_cT[sl] + cc * P * 2, [[1, P]]),
                                           rhs=A16(o_wo + (cc * DM + hf * 512) * 2, [[1, 512]]),
                                           start=(cc == 0), stop=(cc == NDC - 1))
                        return ins
                    pg.op("pe", mm, reads=[t_cT[sl], t_wo], writes=[bankT[bank]])

            def stage_r(k):
                sl = k % 2
                for hf in range(2):
                    bank = (2 * k + hf) % 4
                    xh = A32(o_x[sl] + hf * 2048, [[1, 512]])
                    pg.op("dve", lambda e, bank=bank, xh=xh: e.tensor_tensor(out=xh, in0=PS32(bank, [[1, 512]]), in1=xh, op=ALU.add),
                          reads=[bankT[bank], t_x[sl]], writes=[t_x[sl]])
                dst = DR(x1_d, k * P * DM, [[DM, P], [1, DM]])
                stores.append(pg.op(cfg.get("st2c_eng", "pool"), lambda e: e.dma_start(out=dst, in_=A32(o_x[sl], [[1, DM]])),
                                    reads=[t_x[sl]], lane=l_xs[sl]))

            stage_a(0)
            stage_t(0)
            for k in range(n_sub):
                if k + 1 < n_sub:
                    stage_a(k + 1)
                stage_mm(k)
                if k + 1 < n_sub:
                    stage_t(k + 1)
                stage_r(k)
            sb.release()
            pg.set_fence()
            return stores

        if stop_after == "all":
            sb.mark()
            pre2, load2 = alloc_load_ffn_weights((wg2_d, wu2_d, wd2_d), defer=True)
            st2c = phase2c(st2, OWN // P, after_wo=load2)
            ffn_pass("ffn2", OWN // TT, (wg2_d, wu2_d, wd2_d), 2, x1_d, out_d, final_gain_k=3, src_dep=st2c, pre=pre2)
            sb.release()

        finals = [lst[-1] for dom, lst in pg.dom_ops.items() if dom.startswith("L")]
        pg.op("sp", lambda e: e.nop(), extra_deps=finals)
        pg.emit(nc, stack)
    return nc


def _masks():
    a = np.arange(P)[:, None]
    b = np.arange(P)[None, :]
    m_prev = (b >= a)
    m = np.zeros((P, 4 * P), np.float32)
    m[:, 0:P] = (b <= a)
    m[:, P:2 * P] = (a <= b)
    m[:, 2 * P:3 * P] = (b <= a)
    m[:, 3 * P:4 * P] = (a <= b)
    return m


_CACHE = {}


def _get_nc(cfg_key, cfg):
    if cfg_key not in _CACHE:
        _CACHE[cfg_key] = build_program(cfg)
    return _CACHE[cfg_key]


def make_in_maps(inputs):
    x = np.asarray(inputs["x"], np.float32)
    pos = np.asarray(inputs["positions"], np.int32)
    gains = np.stack([np.asarray(inputs["norm_ffn1"], np.float32)[0], np.asarray(inputs["norm_mix"], np.float32)[0],
                      np.asarray(inputs["norm_ffn2"], np.float32)[0], np.asarray(inputs["norm_final"], np.float32)], 0)
    invf = (1.0 / (10000.0 ** (np.arange(0, 64, 2, dtype=np.float32) / 64.0))).astype(np.float32)
    invf = np.ascontiguousarray(np.broadcast_to(invf[None, :], (P, 32)))
    ident = np.eye(P, dtype=np.float32)
    masks = _masks()
    common = {
        "invf": invf, "ident": ident, "masks": masks, "gains": np.ascontiguousarray(gains),
        "a_sink": np.asarray(inputs["a_sink"], np.float32).reshape(1, 8),
        "w_gate1": np.ascontiguousarray(inputs["w_gate1"][0], dtype=np.float32),
        "w_up1": np.ascontiguousarray(inputs["w_up1"][0], dtype=np.float32),
        "w_down1": np.ascontiguousarray(inputs["w_down1"][0], dtype=np.float32),
        "w_gate2": np.ascontiguousarray(inputs["w_gate2"][0], dtype=np.float32),
        "w_up2": np.ascontiguousarray(inputs["w_up2"][0], dtype=np.float32),
        "w_down2": np.ascontiguousarray(inputs["w_down2"][0], dtype=np.float32),
        "w_in": np.ascontiguousarray(inputs["w_in"][0], dtype=np.float32),
        "w_out": np.ascontiguousarray(inputs["w_out"][0], dtype=np.float32),
    }
    maps = []
    for c in range(8):
        b, hf = c // 2, c % 2
        if hf == 0:
            xs = x[b, 0:LOC]
            ps = pos[b, 0:LOC]
        else:
            xs = x[b, SEQ - LOC:SEQ][::-1]
            ps = pos[b, SEQ - LOC:SEQ][::-1]
        m = dict(common)
        m["x"] = np.ascontiguousarray(xs)
        m["pos"] = np.ascontiguousarray(ps.reshape(LOC // P, P).T)
        maps.append(m)
    return maps


def kernel(**inputs):
    nc = _get_nc("full", {})
    maps = make_in_maps(inputs)
    res = run_bass_kernel_spmd(nc, maps, core_ids=list(range(8)))
    out = np.empty((BATCH, SEQ, DM), np.float32)
    for c in range(8):
        b, hf = c // 2, c % 2
        o = np.asarray(res.results[c]["out"], np.float32)
        if hf == 0:
            out[b, 0:OWN] = o
        else:
            out[b, SEQ - OWN:SEQ] = o[::-1]
    return out
```
